# Optimizing a Trainium2 kernel written in Bass

```python
import math
import jax, jax.numpy as jnp
from jax import lax
import numpy as np


D_MODEL = 1024
BATCH = 8
SEQ = 2048
DEPTH = 2
DEC_BATCH = 128
DEC_SEQ = 8
PAST_LEN = 16384
PAGE_SIZE = 128

N_META = 16
D_MIX = D_MODEL
D_LRU = D_MIX // 2
LRU_BLOCKS = 8
LRU_BW = D_LRU // LRU_BLOCKS
CONV_W = 4
LRU_C = 8.0
GLA_HEADS = 4
GLA_DV = (D_MIX - D_LRU) // GLA_HEADS
GLA_DK = GLA_DV // 2
GLA_RANK = 16
GLA_GATE_NORM = 16.0
GLA_CHUNK = 64
D_FF = 128 * ((8 * D_MODEL // 3 + 127) // 128)
EPS = 1e-6
D_IN = 2 * D_LRU + 2 * GLA_HEADS * GLA_DK + 2 * GLA_HEADS * GLA_DV + GLA_RANK

kernel_name = "hymba_rglru_gla_macaron_step"


def rmsnorm(x, g):
    xf = x.astype(jnp.float32)
    y = xf * lax.rsqrt(jnp.mean(xf * xf, axis=-1, keepdims=True) + EPS)
    return (y * g.astype(jnp.float32)).astype(x.dtype)


def swiglu(x, w_gu, w_down):
    gate, up = jnp.split(x @ w_gu, 2, axis=-1)
    return (jax.nn.silu(gate) * up) @ w_down


def causal_conv(x, buf, w, b):
    L = x.shape[1]
    xp = jnp.concatenate([buf.astype(x.dtype), x], axis=1)
    y = b
    for k in range(CONV_W):
        y = y + w[k] * xp[:, k:k + L]
    return y, xp[:, -(CONV_W - 1):]


def rg_lru(x, h0, pos, wa, ba, wx, bx, lam):
    B, L, _ = x.shape
    xb = x.reshape(B, L, LRU_BLOCKS, LRU_BW)
    r = jax.nn.sigmoid(jnp.einsum('blnc,ncd->blnd', xb, wa).reshape(B, L, D_LRU) + ba)
    i = jax.nn.sigmoid(jnp.einsum('blnc,ncd->blnd', xb, wx).reshape(B, L, D_LRU) + bx)
    log_a = -LRU_C * r.astype(jnp.float32) * jax.nn.softplus(-lam.astype(jnp.float32))
    a = jnp.exp(log_a)
    mult = jnp.sqrt(-jnp.expm1(2.0 * log_a))
    reset = (pos == 0)[None, :, None]
    mult = jnp.where(reset, 1.0, mult)
    a = jnp.where(reset, 0.0, a)
    bt = mult * (i * x).astype(jnp.float32)
    bt = bt.at[:, 0].add(a[:, 0] * h0.astype(jnp.float32))

    def combine(c1, c2):
        a1, b1 = c1
        a2, b2 = c2
        return a1 * a2, a2 * b1 + b2

    _, h = lax.associative_scan(combine, (a, bt), axis=1)
    return h.astype(x.dtype), h[:, -1].astype(x.dtype)


def gla_chunked(q, k, v, g, S0):
    B, L, H, _ = q.shape
    C = math.gcd(L, GLA_CHUNK)
    N = L // C

    def to_chunks(t):
        return t.astype(jnp.float32).reshape(B, N, C, H, t.shape[-1]).transpose(1, 0, 3, 2, 4)

    causal = jnp.tril(jnp.ones((C, C), dtype=bool))

    def step(S, inp):
        qc, kc, vc, gc = inp
        b = jnp.cumsum(gc, axis=2)
        b_last = b[:, :, -1:, :]
        q_s = qc * jnp.exp(b)
        k_s = kc * jnp.exp(-b)
        k_end = kc * jnp.exp(b_last - b)
        att = jnp.where(causal, jnp.einsum('bhik,bhjk->bhij', q_s, k_s), 0.0)
        o = jnp.einsum('bhik,bhkv->bhiv', q_s, S) + jnp.einsum('bhij,bhjv->bhiv', att, vc)
        S = jnp.swapaxes(jnp.exp(b_last), -1, -2) * S + jnp.einsum('bhjk,bhjv->bhkv', k_end, vc)
        return S, o

    S, o = lax.scan(step, S0.astype(jnp.float32), (to_chunks(q), to_chunks(k), to_chunks(v), to_chunks(g)))
    o = o.transpose(1, 0, 3, 2, 4).reshape(B, L, H, GLA_DV)
    return o, S


def mixer(hn, conv_buf, h0, S0, pos, n_meta, w_in, conv_w, conv_b, wa, ba, wx, bx, lam,
          w_gate2, b_gate, gla_norm, w_out):
    B, L, _ = hn.shape
    u = hn @ w_in
    hk = GLA_HEADS * GLA_DK
    hv = GLA_HEADS * GLA_DV
    splits = [D_LRU, 2 * D_LRU, 2 * D_LRU + hk, 2 * D_LRU + 2 * hk,
              2 * D_LRU + 2 * hk + hv, 2 * D_LRU + 2 * hk + 2 * hv]
    xl, gl, q, k, v, go, lr = jnp.split(u, splits, axis=-1)
    xc, new_buf = causal_conv(xl, conv_buf, conv_w, conv_b)
    h, h_last = rg_lru(xc, h0, pos, wa, ba, wx, bx, lam)
    lru_out = h * jax.nn.gelu(gl)
    q = q.reshape(B, L, GLA_HEADS, GLA_DK) * (GLA_DK ** -0.5)
    k = k.reshape(B, L, GLA_HEADS, GLA_DK)
    v = v.reshape(B, L, GLA_HEADS, GLA_DV)
    g = jax.nn.log_sigmoid((lr @ w_gate2 + b_gate).astype(jnp.float32)) / GLA_GATE_NORM
    g = g.reshape(B, L, GLA_HEADS, GLA_DK)
    if n_meta > 0:
        o_m, S = gla_chunked(q[:, :n_meta], k[:, :n_meta], v[:, :n_meta], g[:, :n_meta], S0)
        o_r, S = gla_chunked(q[:, n_meta:], k[:, n_meta:], v[:, n_meta:], g[:, n_meta:], S)
        o = jnp.concatenate([o_m, o_r], axis=1)
    else:
        o, S = gla_chunked(q, k, v, g, S0)
    o = rmsnorm(o.astype(hn.dtype), gla_norm).reshape(B, L, hv)
    gla_out = o * jax.nn.silu(go)
    y = jnp.concatenate([lru_out, gla_out], axis=-1) @ w_out
    return y, new_buf, h_last, S.astype(hn.dtype)


def run_trunk(x, conv0, h0, S0, pos, n_meta, weights):
    (norm_ffn1, w_ffn1_gu, w_ffn1_down, norm_mix, w_in, lru_conv_w, lru_conv_b,
     lru_wa, lru_ba, lru_wx, lru_bx, lru_lambda, gla_w_gate2, gla_b_gate, gla_norm,
     w_out, norm_ffn2, w_ffn2_gu, w_ffn2_down, norm_final) = weights
    convs, hs, Ss = [], [], []
    for l in range(DEPTH):
        x = x + 0.5 * swiglu(rmsnorm(x, norm_ffn1[l]), w_ffn1_gu[l], w_ffn1_down[l])
        m, cb, hl, S = mixer(rmsnorm(x, norm_mix[l]), conv0[l], h0[l], S0[l], pos, n_meta,
                             w_in[l], lru_conv_w[l], lru_conv_b[l], lru_wa[l], lru_ba[l],
                             lru_wx[l], lru_bx[l], lru_lambda[l], gla_w_gate2[l], gla_b_gate[l],
                             gla_norm[l], w_out[l])
        x = x + m
        x = x + 0.5 * swiglu(rmsnorm(x, norm_ffn2[l]), w_ffn2_gu[l], w_ffn2_down[l])
        convs.append(cb)
        hs.append(hl)
        Ss.append(S)
    return rmsnorm(x, norm_final), jnp.stack(hs), jnp.stack(convs), jnp.stack(Ss)


def setup_inputs(seed: int = 0) -> dict:
    key = jax.random.key(seed)
    ks = jax.random.split(key, 40)
    f32 = jnp.float32

    def nrm(k, shape, scale):
        return jax.random.normal(k, shape, f32) * scale

    a0 = jax.random.uniform(ks[14], (DEPTH, D_LRU), f32, minval=0.9, maxval=0.999)
    return {
        "x_prompt": nrm(ks[0], (BATCH, SEQ, D_MODEL), 1.0),
        "x_sample": nrm(ks[1], (DEC_BATCH, DEC_SEQ, D_MODEL), 1.0),
        "state_lru_h": nrm(ks[2], (DEPTH, DEC_BATCH, D_LRU), 0.5),
        "state_lru_conv": nrm(ks[3], (DEPTH, DEC_BATCH, CONV_W - 1, D_LRU), 1.0),
        "state_gla_S": nrm(ks[4], (DEPTH, DEC_BATCH, GLA_HEADS, GLA_DK, GLA_DV), 0.3),
        "meta": nrm(ks[5], (N_META, D_MODEL), 1.0),
        "norm_ffn1": 1.0 + nrm(ks[6], (DEPTH, D_MODEL), 0.02),
        "w_ffn1_gu": nrm(ks[7], (DEPTH, D_MODEL, 2 * D_FF), D_MODEL ** -0.5),
        "w_ffn1_down": nrm(ks[8], (DEPTH, D_FF, D_MODEL), D_FF ** -0.5),
        "norm_mix": 1.0 + nrm(ks[9], (DEPTH, D_MODEL), 0.02),
        "w_in": nrm(ks[10], (DEPTH, D_MODEL, D_IN), D_MODEL ** -0.5),
        "lru_conv_w": nrm(ks[11], (DEPTH, CONV_W, D_LRU), CONV_W ** -0.5),
        "lru_conv_b": nrm(ks[12], (DEPTH, D_LRU), 0.01),
        "lru_wa": nrm(ks[13], (DEPTH, LRU_BLOCKS, LRU_BW, LRU_BW), LRU_BW ** -0.5),
        "lru_ba": nrm(ks[15], (DEPTH, D_LRU), 0.01),
        "lru_wx": nrm(ks[16], (DEPTH, LRU_BLOCKS, LRU_BW, LRU_BW), LRU_BW ** -0.5),
        "lru_bx": nrm(ks[17], (DEPTH, D_LRU), 0.01),
        "lru_lambda": jnp.log(a0) - jnp.log1p(-a0),
        "gla_w_gate2": nrm(ks[18], (DEPTH, GLA_RANK, GLA_HEADS * GLA_DK), GLA_RANK ** -0.5),
        "gla_b_gate": nrm(ks[19], (DEPTH, GLA_HEADS * GLA_DK), 0.1),
        "gla_norm": 1.0 + nrm(ks[20], (DEPTH, GLA_DV), 0.02),
        "w_out": nrm(ks[21], (DEPTH, D_MIX, D_MODEL), D_MIX ** -0.5),
        "norm_ffn2": 1.0 + nrm(ks[22], (DEPTH, D_MODEL), 0.02),
        "w_ffn2_gu": nrm(ks[23], (DEPTH, D_MODEL, 2 * D_FF), D_MODEL ** -0.5),
        "w_ffn2_down": nrm(ks[24], (DEPTH, D_FF, D_MODEL), D_FF ** -0.5),
        "norm_final": 1.0 + nrm(ks[25], (D_MODEL,), 0.02),
    }


def reference(x_prompt, x_sample, state_lru_h, state_lru_conv, state_gla_S, meta,
              norm_ffn1, w_ffn1_gu, w_ffn1_down, norm_mix, w_in, lru_conv_w, lru_conv_b,
              lru_wa, lru_ba, lru_wx, lru_bx, lru_lambda, gla_w_gate2, gla_b_gate, gla_norm,
              w_out, norm_ffn2, w_ffn2_gu, w_ffn2_down, norm_final):
    weights = (norm_ffn1, w_ffn1_gu, w_ffn1_down, norm_mix, w_in, lru_conv_w, lru_conv_b,
               lru_wa, lru_ba, lru_wx, lru_bx, lru_lambda, gla_w_gate2, gla_b_gate, gla_norm,
               w_out, norm_ffn2, w_ffn2_gu, w_ffn2_down, norm_final)
    B, S_len, D = x_prompt.shape
    xp = jnp.concatenate([jnp.broadcast_to(meta.astype(x_prompt.dtype), (B, N_META, D)), x_prompt], axis=1)
    pos_p = jnp.arange(N_META + S_len)
    conv0 = jnp.zeros((DEPTH, B, CONV_W - 1, D_LRU), x_prompt.dtype)
    h0 = jnp.zeros((DEPTH, B, D_LRU), x_prompt.dtype)
    S0 = jnp.zeros((DEPTH, B, GLA_HEADS, GLA_DK, GLA_DV), x_prompt.dtype)
    yp, h_p, conv_p, S_p = run_trunk(xp, conv0, h0, S0, pos_p, N_META, weights)
    y_prompt = yp[:, N_META:]
    pos_s = PAST_LEN + jnp.arange(x_sample.shape[1])
    y_sample, h_s, conv_s, S_s = run_trunk(x_sample, state_lru_conv, state_lru_h, state_gla_S,
                                           pos_s, 0, weights)
    return (y_prompt, y_sample, h_p, conv_p, S_p, h_s, conv_s, S_s)
```

```python
import contextlib
import numpy as np
import concourse.bass as bass
import concourse.mybir as mybir
from concourse.bass_utils import run_bass_kernel_spmd

F32 = mybir.dt.float32
BF16 = mybir.dt.bfloat16
AF = mybir.ActivationFunctionType
ALU = mybir.AluOpType

NCORES = 8
D = 1024
KC = 8
T = 2192
NPROMPT = 2064
TILES = [(0, 400), (400, 512), (912, 512), (1424, 512), (1936, 256)]
DFF = 2816
NJ = 22
GROUPS = [[0, 1, 2, 3], [4, 5, 6, 7], [8, 9, 10, 11], [12, 13, 14, 15], [16, 17, 18], [19, 20, 21]]
R = 11
SLOT = 3072
NMIX = 10
PER_LAYER = NJ + NMIX + NJ
EPS = 1e-6
NCOLS = 136
NCST = 1168
FW = 516


class Eng:
    def __init__(self, name, sem):
        self.name = name
        self.sem = sem
        self.count = 0
        self.seen = {}


class Prog:
    def __init__(self, sems, same_engine_sync=True):
        self.sems = list(sems)
        self.same = same_engine_sync
        self.engs = {}
        self.last_write = {}
        self.readers = {}
        self.dma_cnt = {}
        self.streams = {}
        self.divert = None
        self.raw_only = False
        self.sched_cost = None
        self.sched_mode = 0
        self.tbl_pen = 1300.0

    def add_engine(self, name):
        e = Eng(name, self.sems.pop())
        self.engs[name] = e
        self.streams[name] = []
        return e

    def new_sem(self):
        s = self.sems.pop()
        self.dma_cnt[id(s)] = 0
        return s

    def _deps(self, reads, writes, own_sem=None):
        toks = {}

        def add(t):
            k = id(t[0])
            if k not in toks or toks[k][1] < t[1]:
                toks[k] = t

        for r in reads:
            t = self.last_write.get(r)
            if t is not None:
                add(t)
        for w in writes:
            t = self.last_write.get(w)
            if t is not None and not (self.raw_only and t[0] is own_sem):
                add(t)
            for t in self.readers.get(w, ()):
                if not (self.raw_only and t[0] is own_sem):
                    add(t)
        return list(toks.values())

    def op(self, eng, fn, reads=(), writes=(), dma_sem=None, self_sync=None, grp=None, tbl=None):
        if self.divert is not None:
            self.divert.append((eng, fn, tuple(reads), tuple(writes), dma_sem, self_sync, tbl))
            return None
        e = self.engs[eng]
        same = self.same if self_sync is None else self_sync
        waits = []
        for sem, val in self._deps(reads, writes, e.sem):
            if sem is e.sem and not same:
                continue
            if e.seen.get(id(sem), 0) >= val:
                continue
            e.seen[id(sem)] = val
            waits.append((sem, val))
        if dma_sem is not None:
            self.dma_cnt[id(dma_sem)] += 16
            tok = (dma_sem, self.dma_cnt[id(dma_sem)])
            inc = (dma_sem, 16)
        else:
            e.count += 1
            tok = (e.sem, e.count)
            inc = (e.sem, 1)
        self.streams[eng].append((waits, fn, inc))
        for r in reads:
            self.readers.setdefault(r, []).append(tok)
        for w in writes:
            self.last_write[w] = tok
            self.readers[w] = []
        return tok

    def wait_tok(self, eng, tok):
        e = self.engs[eng]
        sem, val = tok
        if val == 0 or e.seen.get(id(sem), 0) >= val:
            return
        e.seen[id(sem)] = val
        self.streams[eng].append(([(sem, val)], None, None))

    def barrier(self, names=None, dma_sems=()):
        names = names or list(self.engs)
        for n in names:
            e = self.engs[n]
            for m in names:
                o = self.engs[m]
                if o is e or o.count == 0:
                    continue
                self.wait_tok(n, (o.sem, o.count))
            for ds in dma_sems:
                self.wait_tok(n, (ds, self.dma_cnt[id(ds)]))

    def merge(self, lists):
        pos = [0] * len(lists)
        tot = [max(len(l), 1) for l in lists]
        while True:
            best = None
            for i, l in enumerate(lists):
                if pos[i] < len(l):
                    fr = pos[i] / tot[i]
                    if best is None or fr < best[0]:
                        best = (fr, i)
            if best is None:
                break
            i = best[1]
            eng, fn, reads, writes, dma_sem, self_sync, grp = lists[i][pos[i]]
            pos[i] += 1
            self.op(eng, fn, reads=reads, writes=writes, dma_sem=dma_sem, self_sync=self_sync)

    def merge_sched(self, lists, cost=None):
        cost = cost or {"pe": 350.0, "act": 600.0, "dve": 600.0, "pool": 800.0, "sp": 2000.0}
        pos = [0] * len(lists)
        eng_free = {}
        wr_done = {}
        rd_done = {}

        def emit_one(i):
            eng, fn, reads, writes, dma_sem, self_sync, grp = lists[i][pos[i]]
            pos[i] += 1
            st = eng_free.get(eng, 0.0)
            for r in reads:
                st = max(st, wr_done.get(r, 0.0))
            for w in writes:
                st = max(st, wr_done.get(w, 0.0), rd_done.get(w, 0.0))
            fin = st + cost.get(eng, 500.0)
            eng_free[eng] = st + (60.0 if eng == "sp" else cost.get(eng, 500.0))
            for r in reads:
                rd_done[r] = max(rd_done.get(r, 0.0), fin)
            for w in writes:
                wr_done[w] = fin
                rd_done[w] = 0.0
            self.op(eng, fn, reads=reads, writes=writes, dma_sem=dma_sem, self_sync=self_sync)
            return grp

        while True:
            best = None
            for i, l in enumerate(lists):
                if pos[i] >= len(l):
                    continue
                eng, fn, reads, writes, dma_sem, self_sync, grp = l[pos[i]]
                st = eng_free.get(eng, 0.0)
                for r in reads:
                    st = max(st, wr_done.get(r, 0.0))
                for w in writes:
                    st = max(st, wr_done.get(w, 0.0), rd_done.get(w, 0.0))
                key = (st, -(len(l) - pos[i]))
                if best is None or key < best[0]:
                    best = (key, i)
            if best is None:
                break
            i = best[1]
            g = emit_one(i)
            while g is not None and pos[i] < len(lists[i]) and lists[i][pos[i]][6] == g:
                emit_one(i)

    def schedule(self, ops, cost=None):
        cost = cost or self.sched_cost or {"pe": 200.0, "act": 650.0, "dve": 650.0, "pool": 300.0, "sp": 2000.0}
        n = len(ops)
        preds = [set() for _ in range(n)]
        last_w, rdrs = {}, {}
        for i, (eng, fn, reads, writes, dma_sem, self_sync, grp) in enumerate(ops):
            for r in reads:
                if r in last_w:
                    preds[i].add(last_w[r])
            for w in writes:
                if w in last_w:
                    preds[i].add(last_w[w])
                preds[i].update(rdrs.get(w, ()))
            for r in reads:
                rdrs.setdefault(r, []).append(i)
            for w in writes:
                last_w[w] = i
                rdrs[w] = []
            preds[i].discard(i)
        succs = [[] for _ in range(n)]
        indeg = [0] * n
        for i in range(n):
            indeg[i] = len(preds[i])
            for p in preds[i]:
                succs[p].append(i)
        dur = [cost.get(op[0], 500.0) for op in ops]
        cp = [0.0] * n
        for i in range(n - 1, -1, -1):
            m = 0.0
            for sx in succs[i]:
                if cp[sx] > m:
                    m = cp[sx]
            cp[i] = dur[i] + m
        fin = [0.0] * n
        eng_free = {}
        cur_tbl = [None]
        ready = [i for i in range(n) if indeg[i] == 0]
        emitted = 0
        while ready:
            best = None
            for i in ready:
                eng = ops[i][0]
                st = eng_free.get(eng, 0.0)
                for p in preds[i]:
                    if fin[p] > st:
                        st = fin[p]
                if eng == "act" and ops[i][6] is not None and ops[i][6] != cur_tbl[0]:
                    st += self.tbl_pen
                if self.sched_mode == 1:
                    key = (st - 0.25 * cp[i], i)
                elif self.sched_mode == 2:
                    key = (-cp[i], st, i)
                else:
                    key = (st, -cp[i], i)
                if best is None or key < best[0]:
                    best = (key, i, st)
            _, i, st = best
            ready.remove(i)
            eng, fn, reads, writes, dma_sem, self_sync, tbl = ops[i]
            d_ = dur[i]
            if eng == "act" and tbl is not None and tbl != cur_tbl[0]:
                d_ += self.tbl_pen
                cur_tbl[0] = tbl
            fin[i] = st + d_
            eng_free[eng] = st + (60.0 if eng in ("sp", "pool") else d_)
            self.op(eng, fn, reads=reads, writes=writes, dma_sem=dma_sem, self_sync=self_sync)
            emitted += 1
            for sx in succs[i]:
                indeg[sx] -= 1
                if indeg[sx] == 0:
                    ready.append(sx)
        assert emitted == n, (emitted, n)

    def seal(self, sem):
        tot = self.dma_cnt[id(sem)]
        for k, t in list(self.last_write.items()):
            if t[0] is sem:
                self.last_write[k] = (sem, tot)

    def final_wait(self, eng, sems):
        for s in sems:
            self.wait_tok(eng, (s, self.dma_cnt[id(s)]))

    def emit(self, block_map):
        for name, deco in block_map.items():
            stream = self.streams[name]

            def body(h, stream=stream):
                for waits, fn, inc in stream:
                    for sem, val in waits:
                        h.wait_ge(sem, val)
                    if fn is not None:
                        fn(h).then_inc(inc[0], inc[1])

            deco(body)


def build_program(cfg):
    L = cfg.get("layers", 2)
    do_mixer = cfg.get("mixer", True)
    do_ffn2 = cfg.get("ffn2", True)
    final_norm = cfg.get("final_norm", True)
    nslots_total = 2 * PER_LAYER

    nc = bass.Bass("TRN2", target_bir_lowering=False)
    xT_d = nc.dram_tensor("xT", [D, T], F32, kind="ExternalInput").ap()
    w_d = nc.dram_tensor("wslots", [nslots_total, 128, SLOT], F32, kind="ExternalInput").ap()
    cols_d = nc.dram_tensor("cols", [128, NCOLS], F32, kind="ExternalInput").ap()
    wab_d = nc.dram_tensor("wab", [2, 2, 8, 64, 64], F32, kind="ExternalInput").ap()
    wg2_d = nc.dram_tensor("wg2", [2, 16, 256], F32, kind="ExternalInput").ap()
    h0T_d = nc.dram_tensor("h0T", [2, 512, 16], F32, kind="ExternalInput").ap()
    c0T_d = nc.dram_tensor("c0T", [2, 512, 16, 3], F32, kind="ExternalInput").ap()
    S0_d = nc.dram_tensor("S0", [2, 2, 2, 128, 1024], F32, kind="ExternalInput").ap()
    cst_d = nc.dram_tensor("consts", [128, NCST], F32, kind="ExternalInput").ap()
    yT_d = nc.dram_tensor("yT", [D, T], F32, kind="ExternalOutput").ap()
    hTo_d = nc.dram_tensor("hTo", [2, 512, 17], F32, kind="ExternalOutput").ap()
    cTo_d = nc.dram_tensor("cTo", [2, 512, 17, 3], F32, kind="ExternalOutput").ap()
    So_d = nc.dram_tensor("So", [2, 256, 128], F32, kind="ExternalOutput").ap()
    Sos_d = nc.dram_tensor("Sos", [2, 2, 2, 128, 1024], F32, kind="ExternalOutput").ap()

    es = contextlib.ExitStack()
    with es:
        x = es.enter_context(nc.sbuf_tensor("x", [128, KC, T], F32))
        NSCR = 15680
        scr = es.enter_context(nc.sbuf_tensor("scr", [128, NSCR], F32))
        ring = es.enter_context(nc.sbuf_tensor("ring", [128, R, SLOT], BF16))
        cols = es.enter_context(nc.sbuf_tensor("cols_sb", [128, NCOLS], F32))
        ones = es.enter_context(nc.sbuf_tensor("ones", [128, 128], BF16))
        psum = [es.enter_context(nc.psum_tensor(f"ps{i}", [128, 512], F32)) for i in range(8)]
        cst = es.enter_context(nc.sbuf_tensor("cst", [128, NCST], BF16))
        bd = es.enter_context(nc.sbuf_tensor("bd", [128, 16, 128], BF16))
        wg2b = es.enter_context(nc.sbuf_tensor("wg2b", [16, 2, 256], BF16))
        sm = es.enter_context(nc.sbuf_tensor("sm", [128, 160], F32))
        sm2 = es.enter_context(nc.sbuf_tensor("sm2", [128, 32], F32))
        sm3 = es.enter_context(nc.sbuf_tensor("sm3", [128, 96], F32))
        sems = [es.enter_context(nc.semaphore(f"s{i}")) for i in range(52)]
        P = Prog(sems, same_engine_sync=cfg.get("same_sync", True))
        P.raw_only = cfg.get("raw_only", False)
        P.sched_cost = cfg.get("sched_cost")
        P.sched_mode = cfg.get("sched_mode", 0)
        P.tbl_pen = cfg.get("tbl_pen", 1300.0)
        for n in ("pe", "act", "dve", "pool", "sp"):
            P.add_engine(n)
        slot_sem = [P.new_sem() for _ in range(R)]
        in_sem = P.new_sem()
        pin_sem = P.new_sem()
        ob_sem = [P.new_sem(), P.new_sem()]
        osems = {}

        def osem(name):
            if name not in osems:
                osems[name] = P.new_sem()
            return osems[name]
        block = es.enter_context(nc.Block())

        xn = scr[:, 0:4 * T].bitcast(BF16).rearrange("p (k t) -> p k t", k=KC)
        o0 = 4 * T
        hbuf = scr[:, o0:o0 + 2048].bitcast(BF16).rearrange("p (b j n) -> p b j n", b=2, j=4)
        sg = scr[:, o0 + 2048:o0 + 3072].rearrange("p (b n) -> p b n", b=2)
        sq = scr[:, o0 + 3072:o0 + 3584].bitcast(BF16).rearrange("p (b n) -> p b n", b=2)
        rs = scr[:, o0 + 3584:o0 + 4608].rearrange("p (b n) -> p b n", b=2)

        psG = [psum[0], psum[1]]
        psU = [psum[2], psum[3]]
        psD = [psum[4], psum[5], psum[6]]
        psN = psum[7]
        cnt = {"sq": 0, "rs": 0, "g": 0, "u": 0, "d": 0, "sg": 0, "unit": 0, "ob": 0}

        P.op("sp", lambda h: h.dma_start(out=cols[:], in_=cols_d), writes=["cols"], dma_sem=in_sem)
        x_sems = [P.new_sem() for _ in TILES]
        for t, (s_, n_) in enumerate(TILES):
            for kc in range(KC):
                P.op("sp", lambda h, kc=kc, s_=s_, n_=n_: h.dma_start(out=x[:, kc, s_:s_ + n_],
                                                                     in_=xT_d[kc * 128:(kc + 1) * 128, s_:s_ + n_]),
                     writes=[("x", kc, t)], dma_sem=x_sems[t])
            P.seal(x_sems[t])
        P.op("dve", lambda h: h.memset(ones[:], 1.0), writes=["ones"])

        def issue_slot(i, extra_reads=()):
            s = i % R
            P.op("pool",
                 lambda h, i=i, s=s: h.dma_start(out=ring[:, s, :].rearrange("p (a b) -> p a b", a=2),
                                                 in_=w_d[i].rearrange("p (a b) -> p a b", a=2)),
                 reads=list(extra_reads), writes=[("slot", s)], dma_sem=slot_sem[s])

        def full_barrier():
            P.barrier(["pe", "act", "dve", "sp"], dma_sems=ob_sem + list(osems.values()) + [in_sem, pin_sem])

        ffn_sq = [(sq[:, 0, :], ("sq", 0)), (sq[:, 1, :], ("sq", 1))]
        ffn_rs = [(rs[:, 0, :], ("rs", 0)), (rs[:, 1, :], ("rs", 1))]

        def norm_tile(ti, gcol, dst_fn, dst_res, sqb=None, rsb=None, lnexp=False):
            sqb = sqb or ffn_sq
            rsb = rsb or ffn_rs
            s, n = TILES[ti]
            for kc in range(KC):
                sqa, sqr = sqb[cnt["sq"] % 2]
                cnt["sq"] += 1
                P.op("act", lambda h, kc=kc, sqa=sqa: h.activation(out=sqa[:, :n], in_=x[:, kc, s:s + n], func=AF.Square),
                     reads=[("x", kc, ti)], writes=[sqr])
                P.op("pe", lambda h, kc=kc, sqa=sqa: h.matmul(psN[:, :n], lhsT=ones[:], rhs=sqa[:, :n],
                                                              start=(kc == 0), stop=(kc == KC - 1)),
                     reads=[sqr, "ones"], writes=["psN"], self_sync=False)
            rsa, rsr = rsb[cnt["rs"] % 2]
            cnt["rs"] += 1
            if lnexp:
                P.op("act", lambda h: h.activation(out=rsa[:, :n], in_=psN[:, :n], func=AF.Ln, scale=1.0 / D, bias=EPS),
                     reads=["psN"], writes=[rsr], tbl="ln")
                P.op("act", lambda h: h.activation(out=rsa[:, :n], in_=rsa[:, :n], func=AF.Exp, scale=-0.5),
                     reads=[rsr], writes=[rsr], tbl="exp")
            else:
                P.op("act", lambda h: h.activation(out=rsa[:, :n], in_=psN[:, :n], func=AF.Sqrt, scale=1.0 / D, bias=EPS),
                     reads=["psN"], writes=[rsr], tbl="sqrt")
                P.op("dve", lambda h: h.reciprocal(out=rsa[:, :n], in_=rsa[:, :n]),
                     reads=[rsr], writes=[rsr])
            for kc in range(KC):
                P.op("dve", lambda h, kc=kc: h.scalar_tensor_tensor(
                    out=dst_fn(kc, s, n), in0=x[:, kc, s:s + n], scalar=cols[:, gcol + kc:gcol + kc + 1],
                    in1=rsa[:, :n], op0=ALU.mult, op1=ALU.mult),
                    reads=[("x", kc, ti), rsr, "cols"], writes=[dst_res(kc, ti)])

        def ffn_norm(l, f, ti):
            gcol = l * 64 + (0 if f == 0 else 16)
            norm_tile(ti, gcol, lambda kc, s, n: xn[:, kc, s:s + n], lambda kc, ti_: ("xn", kc, ti_))

        def ffn(l, f, slot_base, do_norm=True, after_tile=None):
            units = [(g, ti) for g in range(len(GROUPS)) for ti in range(len(TILES))]

            def phaseA(u):
                g, ti = units[u]
                s, n = TILES[ti]
                hb = u % 2
                for jj, j in enumerate(GROUPS[g]):
                    sl = (slot_base + j) % R
                    gb = cnt["g"] % 2
                    cnt["g"] += 1
                    for kc in range(KC):
                        P.op("pe", lambda h, kc=kc, sl=sl, gb=gb: h.matmul(
                            psG[gb][:, :n], lhsT=ring[:, sl, kc * 128:(kc + 1) * 128], rhs=xn[:, kc, s:s + n],
                            start=(kc == 0), stop=(kc == KC - 1)),
                            reads=[("slot", sl), ("xn", kc, ti)], writes=[("psG", gb)], self_sync=False)
                    for kc in range(KC):
                        P.op("pe", lambda h, kc=kc, sl=sl, gb=gb: h.matmul(
                            psU[gb][:, :n], lhsT=ring[:, sl, 1024 + kc * 128:1024 + (kc + 1) * 128],
                            rhs=xn[:, kc, s:s + n], start=(kc == 0), stop=(kc == KC - 1)),
                            reads=[("slot", sl), ("xn", kc, ti)], writes=[("psU", gb)], self_sync=False)
                    sb = cnt["sg"] % 2
                    cnt["sg"] += 1
                    P.op("act", lambda h, gb=gb, sb=sb: h.activation(out=sg[:, sb, :n], in_=psG[gb][:, :n], func=AF.Silu),
                         reads=[("psG", gb)], writes=[("sg", sb)])
                    P.op("dve", lambda h, gb=gb, sb=sb, jj=jj: h.tensor_tensor(
                        out=hbuf[:, hb, jj, :n], in0=psU[gb][:, :n], in1=sg[:, sb, :n], op=ALU.mult),
                        reads=[("psU", gb), ("sg", sb)], writes=[("h", hb, jj)])

            def phaseB(u):
                g, ti = units[u]
                s, n = TILES[ti]
                hb = u % 2
                ng = len(GROUPS[g])
                for mc in range(KC):
                    db = cnt["d"] % 3
                    cnt["d"] += 1
                    for jj, j in enumerate(GROUPS[g]):
                        sl = (slot_base + j) % R
                        P.op("pe", lambda h, mc=mc, sl=sl, db=db, jj=jj: h.matmul(
                            psD[db][:, :n], lhsT=ring[:, sl, 2048 + mc * 128:2048 + (mc + 1) * 128],
                            rhs=hbuf[:, hb, jj, :n], start=(jj == 0), stop=(jj == ng - 1)),
                            reads=[("slot", sl), ("h", hb, jj)], writes=[("psD", db)], self_sync=False)
                    P.op("dve", lambda h, mc=mc, db=db: h.scalar_tensor_tensor(
                        out=x[:, mc, s:s + n], in0=psD[db][:, :n], scalar=0.5, in1=x[:, mc, s:s + n],
                        op0=ALU.mult, op1=ALU.add),
                        reads=[("psD", db), ("x", mc, ti)], writes=[("x", mc, ti)])

            NT_ = len(TILES)
            NA_ = cfg.get("norm_ahead", 2)
            if do_norm:
                for t_ in range(min(NA_ + 1, NT_)):
                    ffn_norm(l, f, t_)
            phaseA(0)
            pending = []
            for u in range(len(units)):
                if u + 1 < len(units):
                    g1, t1 = units[u + 1]
                    lst = []
                    for pl in pending:
                        lst.extend(pl)
                    pending = []
                    P.divert = lst
                    if do_norm and g1 == 0 and t1 + NA_ < NT_:
                        ffn_norm(l, f, t1 + NA_)
                    phaseA(u + 1)
                    P.divert = None
                    if cfg.get("ffn_sched", True):
                        P.schedule(lst)
                    else:
                        P.merge([lst])
                phaseB(u)
                g, ti = units[u]
                if ti == len(TILES) - 1:
                    released(slot_base + GROUPS[g][-1] + 1)
                if g == len(GROUPS) - 1 and after_tile is not None:
                    if u + 2 < len(units):
                        tmp = []
                        P.divert = tmp
                        after_tile(ti)
                        P.divert = None
                        pending.append(tmp)
                    else:
                        for pl in pending:
                            P.merge([pl])
                        pending = []
                        after_tile(ti)

        ident = cst[:, 0:128]
        maskT = cst[:, 128:256]
        smask = cst[:, 256:384]
        blk16 = cst[:, 384:400]
        Mmask = cst[:, 400:1040]
        m8 = cst[:, 1040:1168]
        xnt2 = [scr[:, 0:2048].bitcast(BF16).rearrange("p (k n) -> p k n", k=KC)]
        mixin = scr[:, 2048:4096].bitcast(BF16).rearrange("p (k n) -> p k n", k=KC)
        Fb = [scr[:, 4096 + i * FW:4096 + (i + 1) * FW] for i in range(12)]
        ob_ = 4096 + 12 * FW
        Bb = [scr[:, ob_ + i * 256:ob_ + (i + 1) * 256].bitcast(BF16) for i in range(6)]
        ob_ += 6 * 256
        vtm = scr[:, ob_:ob_ + 1024].bitcast(BF16).rearrange("p (c v) -> p c v", c=4)
        ob_ += 1024
        kendT = [scr[:, ob_ + i * 256:ob_ + (i + 1) * 256].bitcast(BF16).rearrange("p (c v) -> p c v", c=4) for i in range(2)]
        ob_ += 512
        attT = scr[:, ob_:ob_ + 128].bitcast(BF16).rearrange("p (b v) -> p b v", b=2)
        ob_ += 128
        Sf = scr[:, ob_:ob_ + 256].rearrange("p (b v) -> p b v", b=2)
        ob_ += 256
        Sb = scr[:, ob_:ob_ + 128].bitcast(BF16).rearrange("p (b v) -> p b v", b=2)
        ob_ += 128
        lrT = scr[:, ob_:ob_ + 256].bitcast(BF16)
        ob_ += 256
        S0f = scr[:, ob_:ob_ + 1024].rearrange("p (s v) -> p s v", s=8)
        ob_ += 1024
        S0b = scr[:, ob_:ob_ + 512].bitcast(BF16).rearrange("p (s v) -> p s v", s=8)
        ob_ += 512
        kblk = S0b
        assert ob_ <= NSCR, ob_
        F = lambda i: ("F", i)
        B = lambda i: ("B", i)
        C1, C2, SPT, HIST, HPREV, BLC, EBL, TMP16 = 0, 8, 16, 64, 76, 80, 100, 120
        mU = [psum[0], psum[1]]
        mG = [psum[2], psum[3]]
        mO = [psum[4], psum[5]]
        mV = psum[6]
        mVb = psum[6][:, :].bitcast(BF16)
        mcnt = {"u": 0, "g": 0, "att": 0}

        def mixer_setup():
            P.op("pool", lambda h: h.dma_start(out=cst[:], in_=cst_d), writes=["cst"], dma_sem=pin_sem)
            P.op("dve", lambda h: h.memset(bd[:], 0.0), writes=["bd"])
            for l in range(2):
                for wh in range(2):
                    for c in range(4):
                        for half in range(2):
                            P.op("pool", lambda h, l=l, wh=wh, c=c, half=half: h.dma_start(
                                out=bd[half * 64:(half + 1) * 64, l * 8 + wh * 4 + c, half * 64:(half + 1) * 64],
                                in_=wab_d[l, wh, 2 * c + half]), reads=["bd"], writes=[("bdp", l, wh, c, half)], dma_sem=pin_sem)
                P.op("pool", lambda h, l=l: h.dma_start(out=wg2b[:, l, :], in_=wg2_d[l]), writes=[("wg2bp", l)], dma_sem=pin_sem)
            P.seal(in_sem)
            P.seal(pin_sem)
            P.last_write["bd"] = (pin_sem, P.dma_cnt[id(pin_sem)])
            P.readers["bd"] = []
            P.last_write["wg2b"] = (pin_sem, P.dma_cnt[id(pin_sem)])
            e_ = sm[:, SPT:SPT + 8]
            ser = sm[:, SPT + 8:SPT + 16]
            lnv = sm[:, SPT + 16:SPT + 24]
            msk = sm[:, SPT + 24:SPT + 32]
            for l in range(2):
                P.op("act", lambda h, l=l: h.activation(out=sm[:, SPT + l * 4:SPT + l * 4 + 4], in_=cols[:, l * 64 + 52:l * 64 + 56],
                                                        func=AF.Exp, scale=-1.0), reads=["cols"], writes=["spt"], tbl="exp")
            P.op("act", lambda h: h.activation(out=lnv, in_=e_, func=AF.Ln, bias=1.0, scale=1.0), reads=["spt"], writes=["spt"])
            P.op("dve", lambda h: h.tensor_scalar(out=ser, in0=e_, scalar1=-1.0 / 6, scalar2=1.0 / 5, op0=ALU.mult, op1=ALU.add),
                 reads=["spt"], writes=["spt"])
            for cf in (1.0 / 4, 1.0 / 3, 1.0 / 2, 1.0):
                P.op("dve", lambda h: h.tensor_tensor(out=ser, in0=ser, in1=e_, op=ALU.mult), reads=["spt"], writes=["spt"])
                P.op("dve", lambda h, cf=cf: h.tensor_scalar(out=ser, in0=ser, scalar1=-1.0, scalar2=cf, op0=ALU.mult, op1=ALU.add),
                     reads=["spt"], writes=["spt"])
            P.op("dve", lambda h: h.tensor_tensor(out=ser, in0=ser, in1=e_, op=ALU.mult), reads=["spt"], writes=["spt"])
            P.op("dve", lambda h: h.tensor_single_scalar(out=msk, in_=e_, scalar=0.25, op=ALU.is_lt), reads=["spt"], writes=["spt"])
            P.op("dve", lambda h: h.tensor_tensor(out=ser, in0=ser, in1=lnv, op=ALU.subtract), reads=["spt"], writes=["spt"])
            P.op("dve", lambda h: h.tensor_tensor(out=ser, in0=ser, in1=msk, op=ALU.mult), reads=["spt"], writes=["spt"])
            P.op("dve", lambda h: h.tensor_tensor(out=ser, in0=ser, in1=lnv, op=ALU.add), reads=["spt"], writes=["spt"])
            P.op("dve", lambda h: h.tensor_scalar(out=sm[:, C1:C1 + 8], in0=ser, scalar1=-8.0, scalar2=None, op0=ALU.mult),
                 reads=["spt"], writes=["dcol"])
            P.op("dve", lambda h: h.tensor_scalar(out=sm[:, C2:C2 + 8], in0=ser, scalar1=-16.0, scalar2=None, op0=ALU.mult),
                 reads=["spt"], writes=["dcol"])
            for l in range(2):
                P.op("dve", lambda h, l=l: h.tensor_scalar(out=sm2[:, l * 4:l * 4 + 4], in0=cols[:, l * 64 + 44:l * 64 + 48],
                                                           scalar1=0.5, scalar2=None, op0=ALU.mult), reads=["cols"], writes=["sm2"])
                P.op("dve", lambda h, l=l: h.tensor_scalar(out=sm2[:, 8 + l * 4:8 + l * 4 + 4], in0=cols[:, l * 64 + 48:l * 64 + 52],
                                                           scalar1=0.5, scalar2=None, op0=ALU.mult), reads=["cols"], writes=["sm2"])
                P.op("dve", lambda h, l=l: h.tensor_scalar(out=sm2[:, 24 + l * 2:24 + l * 2 + 2], in0=cols[:, l * 64 + 56:l * 64 + 58],
                                                           scalar1=0.5, scalar2=None, op0=ALU.mult), reads=["cols"], writes=["sm2"])
            P.op("dve", lambda h: h.tensor_scalar(out=sm2[:, 16:24], in0=sm[:, C1:C1 + 8], scalar1=0.5, scalar2=None, op0=ALU.mult),
                 reads=["dcol"], writes=["sm2"])

        def mixer(l, mb):
            cb = l * 64

            def wsl(idx):
                sl_, pos = divmod(idx, 3)
                return (mb + sl_) % R, pos * 1024

            P.divert = []
            P.op("dve", lambda h: h.memset(sm[:, HIST:HIST + 16], 0.0), writes=["hist", "hprev"])
            P.op("dve", lambda h: h.memset(Sf[:], 0.0), writes=["Sf"])
            P.op("dve", lambda h: h.memset(Sb[:], 0.0), writes=["Sb"])
            P.op("dve", lambda h: h.memset(Bb[1][:], 0.0), writes=[B(1)])
            P.op("dve", lambda h: h.memset(Bb[2][:], 0.0), writes=[B(2)])

            for ti in range(len(TILES)):
                mixer_tile(l, mb, cb, wsl, ti, None)
            ops = P.divert
            P.divert = None
            nfill = cfg.get("fill", 0)
            for k_ in range(nfill):
                ops.append(("pe", (lambda h: h.matmul(psum[2][:, :512], lhsT=ones[:], rhs=cst[:, 400:912], start=True, stop=True)),
                            ("cst", "ones"), (("fill", k_),), None, False, None))
            if cfg.get("zip", True):
                P.schedule(ops)
            else:
                P.merge([ops])
            released(mb + NMIX)

        def mixer_tile(l, mb, cb, wsl, ti, pendW):
            s, n = TILES[ti]
            last = (ti == len(TILES) - 1)
            npr = 128 if last else n
            if ti == 0:
                chunks = [(0, 16), (16, 128), (144, 128), (272, 128)]
            elif last:
                chunks = [(0, 128)]
            else:
                chunks = [(i * 128, 128) for i in range(4)]
            cm = Mmask[:, 112:112 + n] if ti == 0 else Mmask[:, 0:npr]

            xpar = 0
            xnt = xnt2[0]
            norm_tile(ti, cb + 8, lambda kc, s_, n_: xnt[:, kc, :n_], lambda kc, ti_: ("xnt", xpar, kc),
                      sqb=[(Bb[4], B(4)), (Bb[5], B(5))], rsb=[(Fb[5], F(5)), (Fb[6], F(6))], lnexp=False)

            def u_mm(idx, m=128, lr=False, ub=0):
                rsl, off = wsl(idx)
                for kc in range(KC):
                    if lr:
                        lhs = ring[:, rsl, 2048 + kc * 16:2048 + kc * 16 + 16]
                    else:
                        lhs = ring[:, rsl, off + kc * 128:off + (kc + 1) * 128]
                    P.op("pe", lambda h, kc=kc, lhs=lhs: h.matmul(mU[ub][:m, :n], lhsT=lhs, rhs=xnt[:, kc, :n],
                                                                 start=(kc == 0), stop=(kc == KC - 1)),
                         reads=[("slot", rsl), ("xnt", xpar, kc)], writes=[("mU", ub)], self_sync=False)
                return mU[ub], ("mU", ub)

            def lru_chunk(c):
                xp, xps, xc, ra, it, am, hs = Fb[0], Fb[1], Fb[2], Fb[3], Fb[4], Fb[5], Fb[6]
                ps, psr = u_mm(c, ub=0)
                wcol = lambda k: cols[:, cb + 24 + k * 4 + c:cb + 25 + k * 4 + c]
                bcol = cols[:, cb + 40 + c:cb + 41 + c]
                P.op("dve", lambda h: h.tensor_copy(out=xp[:, 0:3], in_=sm[:, HIST + c * 3:HIST + c * 3 + 3]),
                     reads=["hist"], writes=[F(0)])
                P.op("act", lambda h: h.activation(out=xp[:, 3:3 + npr], in_=ps[:, 0:npr], func=AF.Copy),
                     reads=[psr], writes=[F(0)])
                if last:
                    xps3 = xps[:, 0:176].rearrange("p (s k) -> p s k", k=11)
                    P.op("act", lambda h: h.activation(out=xps3[:, :, 3:11], in_=ps[:, 128:256].rearrange("p (s k) -> p s k", k=8),
                                                       func=AF.Copy), reads=[psr], writes=[F(1)])
                    P.op("sp", lambda h: h.dma_start(out=xps3[:, :, 0:3], in_=c0T_d[l, c * 128:(c + 1) * 128, :, :]),
                         writes=[F(1)], dma_sem=osem("c0ld"))
                    P.op("sp", lambda h: h.dma_start(out=xps[:, 176:192], in_=h0T_d[l, c * 128:(c + 1) * 128, :]),
                         writes=[F(1)], dma_sem=osem("c0ld"))
                P.op("dve", lambda h: h.tensor_scalar(out=xc[:, 0:npr], in0=xp[:, 0:npr], scalar1=wcol(0), scalar2=bcol,
                                                      op0=ALU.mult, op1=ALU.add), reads=[F(0), "cols"], writes=[F(2)])
                for k in range(1, 4):
                    P.op("dve", lambda h, k=k: h.scalar_tensor_tensor(out=xc[:, 0:npr], in0=xp[:, k:k + npr], scalar=wcol(k),
                                                                      in1=xc[:, 0:npr], op0=ALU.mult, op1=ALU.add),
                         reads=[F(0), F(2), "cols"], writes=[F(2)])
                if last:
                    xcs = xc[:, 128:256].rearrange("p (s k) -> p s k", k=8)
                    P.op("dve", lambda h: h.tensor_scalar(out=xcs, in0=xps3[:, :, 0:8], scalar1=wcol(0), scalar2=bcol,
                                                          op0=ALU.mult, op1=ALU.add), reads=[F(1), "cols"], writes=[F(2)])
                    for k in range(1, 4):
                        P.op("dve", lambda h, k=k: h.scalar_tensor_tensor(out=xcs, in0=xps3[:, :, k:k + 8], scalar=wcol(k),
                                                                          in1=xcs, op0=ALU.mult, op1=ALU.add),
                             reads=[F(1), F(2), "cols"], writes=[F(2)])
                    P.op("sp", lambda h: h.dma_start(out=cTo_d[l, c * 128:(c + 1) * 128, 0, :], in_=xp[:, npr:npr + 3]),
                         reads=[F(0)], dma_sem=osem("F0"))
                    P.op("sp", lambda h: h.dma_start(out=cTo_d[l, c * 128:(c + 1) * 128, 1:17, :], in_=xps3[:, :, 8:11]),
                         reads=[F(1)], dma_sem=osem("F1"))
                else:
                    P.op("dve", lambda h: h.tensor_copy(out=sm[:, HIST + c * 3:HIST + c * 3 + 3], in_=xp[:, n:n + 3]),
                         reads=[F(0)], writes=["hist"])
                P.op("act", lambda h: h.activation(out=Bb[0][:, :n], in_=xc[:, :n], func=AF.Copy), reads=[F(2)], writes=[B(0)])
                P.op("pe", lambda h: h.matmul(mG[0][:, :n], lhsT=bd[:, l * 8 + 0 * 4 + c, :], rhs=Bb[0][:, :n], start=True, stop=True),
                     reads=[B(0), "bd"], writes=[("mG", 0)], self_sync=False)
                P.op("act", lambda h: h.activation(out=ra[:, :n], in_=mG[0][:, :n], func=AF.Tanh, scale=0.5,
                                                   bias=sm2[:, l * 4 + c:l * 4 + c + 1]), reads=[("mG", 0), "sm2"], writes=[F(3)], tbl="exp")
                P.op("pe", lambda h: h.matmul(mG[0][:, :n], lhsT=bd[:, l * 8 + 1 * 4 + c, :], rhs=Bb[0][:, :n], start=True, stop=True),
                     reads=[B(0), "bd"], writes=[("mG", 0)], self_sync=False)
                P.op("act", lambda h: h.activation(out=it[:, :n], in_=mG[0][:, :n], func=AF.Tanh, scale=0.5,
                                                   bias=sm2[:, 8 + l * 4 + c:8 + l * 4 + c + 1]), reads=[("mG", 0), "sm2"], writes=[F(4)], tbl="exp")
                P.op("act", lambda h: h.activation(out=am[:, :n], in_=ra[:, :n], func=AF.Exp,
                                                   scale=sm[:, C1 + l * 4 + c:C1 + l * 4 + c + 1],
                                                   bias=sm[:, C1 + l * 4 + c:C1 + l * 4 + c + 1]), reads=[F(3), "dcol"], writes=[F(5)], tbl="exp")
                P.op("act", lambda h: h.activation(out=ra[:, :n], in_=ra[:, :n], func=AF.Exp,
                                                   scale=sm2[:, 16 + l * 4 + c:16 + l * 4 + c + 1],
                                                   bias=sm2[:, 16 + l * 4 + c:16 + l * 4 + c + 1]), reads=[F(3), "sm2"], writes=[F(3)], tbl="exp")
                P.op("act", lambda h: h.activation(out=am[:, :n], in_=am[:, :n], func=AF.Sqrt, scale=-0.25, bias=0.25),
                     reads=[F(5)], writes=[F(5)], tbl="sqrt")
                P.op("dve", lambda h: h.scalar_tensor_tensor(out=it[:, :n], in0=it[:, :n], scalar=1.0, in1=xc[:, :n],
                                                             op0=ALU.add, op1=ALU.mult),
                     reads=[F(4), F(2)], writes=[F(4)])
                if ti == 0:
                    P.op("dve", lambda h: h.memset(am[:, 0:1], 0.5), reads=[], writes=[F(5)])
                P.op("dve", lambda h: h.tensor_tensor(out=it[:, :n], in0=it[:, :n], in1=am[:, :n], op=ALU.mult),
                     reads=[F(4), F(5)], writes=[F(4)])
                if last:
                    a_st = ra[:, 128:256].rearrange("p (s k) -> p s k", k=8)[:, :, 0]
                    b_st = it[:, 128:256].rearrange("p (s k) -> p s k", k=8)[:, :, 0]
                    t16 = sm[:, TMP16:TMP16 + 16]
                    P.op("dve", lambda h: h.tensor_tensor(out=t16, in0=a_st, in1=xps[:, 176:192], op=ALU.mult),
                         reads=[F(3), F(1)], writes=["t16"])
                    P.op("dve", lambda h: h.tensor_tensor(out=b_st, in0=b_st, in1=t16, op=ALU.add),
                         reads=[F(4), "t16"], writes=[F(4)])
                    P.op("dve", lambda h: h.memset(a_st, 0.0), reads=[], writes=[F(3)])
                init = 0.0 if ti == 0 else sm[:, HPREV + c:HPREV + c + 1]
                P.op("dve", lambda h: h.tensor_tensor_scan(out=hs[:, :n], data0=ra[:, :n], data1=it[:, :n], initial=init,
                                                           op0=ALU.mult, op1=ALU.add),
                     reads=[F(3), F(4), "hprev"], writes=[F(6)])
                if last:
                    P.op("dve", lambda h: h.tensor_copy(out=sm[:, 137:138], in_=hs[:, 127:128]), reads=[F(6)], writes=["hs16"])
                    P.op("dve", lambda h: h.tensor_copy(out=sm[:, 138:154],
                                                        in_=hs[:, 128:256].rearrange("p (s k) -> p s k", k=8)[:, :, 7]),
                         reads=[F(6)], writes=["hs16"])
                    P.op("sp", lambda h: h.dma_start(out=hTo_d[l, c * 128:(c + 1) * 128, :], in_=sm[:, 137:154]),
                         reads=["hs16"], dma_sem=osem("hs16"))
                else:
                    P.op("dve", lambda h: h.tensor_copy(out=sm[:, HPREV + c:HPREV + c + 1], in_=hs[:, n - 1:n]),
                         reads=[F(6)], writes=["hprev"])
                ps2, ps2r = u_mm(4 + c, ub=0)
                P.op("act", lambda h: h.activation(out=xp[:, :n], in_=ps2[:, :n], func=AF.Square), reads=[ps2r], writes=[F(0)])
                P.op("dve", lambda h: h.tensor_scalar(out=xp[:, :n], in0=xp[:, :n], scalar1=0.044715, scalar2=1.0,
                                                      op0=ALU.mult, op1=ALU.add), reads=[F(0)], writes=[F(0)])
                P.op("dve", lambda h: h.tensor_tensor(out=xp[:, :n], in0=ps2[:, :n], in1=xp[:, :n], op=ALU.mult),
                     reads=[F(0), ps2r], writes=[F(0)])
                P.op("act", lambda h: h.activation(out=xp[:, :n], in_=xp[:, :n], func=AF.Tanh, scale=0.7978845608028654),
                     reads=[F(0)], writes=[F(0)], tbl="exp")
                P.op("dve", lambda h: h.scalar_tensor_tensor(out=xp[:, :n], in0=xp[:, :n], scalar=1.0, in1=ps2[:, :n],
                                                             op0=ALU.add, op1=ALU.mult), reads=[F(0), ps2r], writes=[F(0)])
                P.op("dve", lambda h: h.scalar_tensor_tensor(out=mixin[:, c, :n], in0=xp[:, :n], scalar=0.5, in1=hs[:, :n],
                                                             op0=ALU.mult, op1=ALU.mult),
                     reads=[F(6), F(0)], writes=[("mixin", c)])


            psl, pslr = u_mm(18, m=16, lr=True, ub=1)
            P.op("act", lambda h: h.activation(out=lrT[0:16, :n], in_=psl[0:16, :n], func=AF.Copy), reads=[pslr], writes=["lrT"])
            vchunks = chunks + ([(128, 128)] if last else [])
            for ch, (cs, cl) in enumerate(vchunks):
                for pr in range(2):
                    rsl = (mb + 4 + pr) % R
                    rv = ring[:, rsl, :].rearrange("p (c k m) -> p c k m", c=3, k=KC)
                    for kc in range(KC):
                        P.op("pe", lambda h, kc=kc, pr=pr, rv=rv, cs=cs, cl=cl: h.matmul(
                            mV[:cl, pr * 256:(pr + 1) * 256], lhsT=xnt[:, kc, cs:cs + cl], rhs=rv[:, 0:2, kc, :],
                            start=(kc == 0), stop=(kc == KC - 1)),
                            reads=[("slot", rsl), ("xnt", xpar, kc)], writes=["mV"], self_sync=False)
                P.op("act", lambda h, ch=ch, cl=cl: h.activation(out=vtm[:cl, ch, :], in_=mV[:cl, :], func=AF.Copy),
                     reads=["mV"], writes=[("vtm", ch)])

            def gla_pair(p):
                BLCp, EBLp = p * 48, p * 48 + 24
                gg, b16, eb, enb, ek = Fb[7], Fb[8], Fb[9], Fb[10], Fb[11]
                qsA, qsB, ks, kend = Bb[1], Bb[2], Bb[3], Bb[4]
                P.op("pe", lambda h: h.matmul(mU[1][:, :n], lhsT=wg2b[:, l, p * 128:(p + 1) * 128], rhs=lrT[0:16, :n],
                                              start=True, stop=True), reads=["wg2b", "lrT"], writes=[("mU", 1)], self_sync=False)
                P.op("act", lambda h: h.activation(out=gg[:, :n], in_=mU[1][:, :n], func=AF.Tanh, scale=0.5,
                                                   bias=sm2[:, 24 + l * 2 + p:24 + l * 2 + p + 1]), reads=[("mU", 1), "sm2"], writes=[F(7)], tbl="exp")
                P.op("act", lambda h: h.activation(out=gg[:, :n], in_=gg[:, :n], func=AF.Ln, scale=0.5, bias=0.5),
                     reads=[F(7)], writes=[F(7)], tbl="ln")
                P.op("dve", lambda h: h.tensor_tensor_scan(out=b16[:, :npr], data0=cm, data1=gg[:, :npr], initial=0.0,
                                                           op0=ALU.mult, op1=ALU.add), reads=[F(7), "cst"], writes=[F(8)])
                if last:
                    P.op("dve", lambda h: h.tensor_tensor_scan(out=b16[:, 128:256], data0=m8, data1=gg[:, 128:256], initial=0.0,
                                                               op0=ALU.mult, op1=ALU.add), reads=[F(7), "cst"], writes=[F(8)])
                P.op("act", lambda h: h.activation(out=eb[:, :n], in_=b16[:, :n], func=AF.Exp, scale=1.0 / 16),
                     reads=[F(8)], writes=[F(9)], tbl="exp")
                P.op("act", lambda h: h.activation(out=enb[:, :n], in_=b16[:, :n], func=AF.Exp, scale=-1.0 / 16),
                     reads=[F(8)], writes=[F(10)], tbl="exp")
                for ch, (cs, cl) in enumerate(chunks):
                    P.op("dve", lambda h, ch=ch, cs=cs, cl=cl: h.tensor_scalar(
                        out=sm3[:, BLCp + ch:BLCp + ch + 1], in0=b16[:, cs + cl - 1:cs + cl], scalar1=1.0 / 16, scalar2=None,
                        op0=ALU.mult), reads=[F(8)], writes=[("blc", p)])
                    P.op("act", lambda h, ch=ch, cs=cs, cl=cl: h.activation(
                        out=ek[:, cs:cs + cl], in_=b16[:, cs:cs + cl], func=AF.Exp, scale=-1.0 / 16,
                        bias=sm3[:, BLCp + ch:BLCp + ch + 1]), reads=[F(8), ("blc", p)], writes=[F(11)], tbl="exp")
                    P.op("act", lambda h, ch=ch: h.activation(out=sm3[:, EBLp + ch:EBLp + ch + 1], in_=sm3[:, BLCp + ch:BLCp + ch + 1],
                                                              func=AF.Exp), reads=[("blc", p)], writes=[("ebl", p)], tbl="exp")
                if last:
                    bls = sm3[:, BLCp + 4:BLCp + 20]
                    ebls = sm3[:, EBLp + 4:EBLp + 20]
                    P.op("dve", lambda h: h.tensor_scalar(out=bls, in0=b16[:, 128:256].rearrange("p (s k) -> p s k", k=8)[:, :, 7],
                                                          scalar1=1.0 / 16, scalar2=None, op0=ALU.mult), reads=[F(8)], writes=[("blc", p)])
                    P.op("act", lambda h: h.activation(out=ebls, in_=bls, func=AF.Exp), reads=[("blc", p)], writes=[("ebl", p)], tbl="exp")
                    P.op("dve", lambda h: h.tensor_tensor(out=ek[:, 128:256].rearrange("p (s k) -> p s k", k=8),
                                                          in0=enb[:, 128:256].rearrange("p (s k) -> p s k", k=8),
                                                          in1=ebls.unsqueeze(2).to_broadcast([128, 16, 8]), op=ALU.mult),
                         reads=[F(10), ("ebl", p)], writes=[F(11)])
                psq, psqr = u_mm(8 + p, ub=1)
                P.op("dve", lambda h: h.scalar_tensor_tensor(out=qsA[0:64, :n], in0=psq[0:64, :n], scalar=0.125, in1=eb[0:64, :n],
                                                             op0=ALU.mult, op1=ALU.mult), reads=[psqr, F(9)], writes=[B(1)])
                P.op("dve", lambda h: h.scalar_tensor_tensor(out=qsB[64:128, :n], in0=psq[64:128, :n], scalar=0.125,
                                                             in1=eb[64:128, :n], op0=ALU.mult, op1=ALU.mult),
                     reads=[psqr, F(9)], writes=[B(2)])
                psk, pskr = u_mm(10 + p, ub=1)
                P.op("dve", lambda h: h.tensor_tensor(out=ks[:, :n], in0=psk[:, :n], in1=enb[:, :n], op=ALU.mult),
                     reads=[pskr, F(10)], writes=[B(3)])
                P.op("dve", lambda h: h.tensor_tensor(out=kend[:, :n], in0=psk[:, :n], in1=ek[:, :n], op=ALU.mult),
                     reads=[pskr, F(11)], writes=[B(4)])
                for ch, (cs, cl) in enumerate(vchunks):
                    P.op("pe", lambda h, ch=ch, cs=cs, cl=cl: h.transpose(out=mVb[:cl, ch * 128:(ch + 1) * 128],
                                                                          in_=kend[:, cs:cs + cl], identity=ident),
                         reads=[B(4), "cst"], writes=["mV"], self_sync=False)
                c0_ = 0
                if vchunks[0][1] < 128:
                    cl0 = vchunks[0][1]
                    P.op("act", lambda h: h.activation(out=kendT[p][:cl0, 0, :], in_=mVb[:cl0, 0:128], func=AF.Copy),
                         reads=["mV"], writes=[("kendT", p)])
                    c0_ = 1
                P.op("act", lambda h: h.activation(out=kendT[p][:, c0_:len(vchunks), :],
                                                   in_=mVb[:, c0_ * 128:len(vchunks) * 128].rearrange("p (c v) -> p c v", v=128),
                                                   func=AF.Copy), reads=["mV"], writes=[("kendT", p)])
                for ch, (cs, cl) in enumerate(chunks):
                    for hd in range(2):
                        h4 = 2 * p + hd
                        qs = qsA if hd == 0 else qsB
                        qr = B(1) if hd == 0 else B(2)
                        ab = mcnt["att"] % 2
                        mcnt["att"] += 1
                        gb2 = mcnt["g"] % 2
                        mcnt["g"] += 1
                        attb, attr = ((mG[1], ("mG", 1)), (mV, "mV"))[gb2]
                        P.op("pe", lambda h, cs=cs, cl=cl, qs=qs, attb=attb: h.matmul(
                            attb[:cl, :cl], lhsT=ks[:, cs:cs + cl], rhs=qs[:, cs:cs + cl], start=True, stop=True),
                            reads=[B(3), qr], writes=[attr], self_sync=False)
                        P.op("dve", lambda h, cl=cl, ab=ab, attb=attb: h.tensor_tensor(
                            out=attT[:cl, ab, :cl], in0=attb[:cl, :cl], in1=maskT[:cl, :cl], op=ALU.mult),
                            reads=[attr, "cst"], writes=[("attT", ab)])
                        P.op("pe", lambda h, cs=cs, cl=cl, qs=qs, hd=hd: h.matmul(
                            mO[hd][:, cs:cs + cl], lhsT=Sb[:, p, :], rhs=qs[:, cs:cs + cl], start=True, stop=False),
                            reads=["Sb", qr], writes=[("mO", hd)], self_sync=False)
                        P.op("pe", lambda h, cs=cs, cl=cl, ch=ch, ab=ab, hd=hd, h4=h4: h.matmul(
                            mO[hd][:, cs:cs + cl], lhsT=vtm[:cl, ch, h4 * 128:(h4 + 1) * 128], rhs=attT[:cl, ab, :cl],
                            start=False, stop=True),
                            reads=[("vtm", ch), ("attT", ab)], writes=[("mO", hd)], self_sync=False)
                    P.op("pe", lambda h, cl=cl, ch=ch: h.matmul(
                        psN[:, 0:256], lhsT=kendT[p][:cl, ch, :], rhs=vtm[:cl, ch, p * 256:(p + 1) * 256],
                        start=True, stop=True), reads=[("kendT", p), ("vtm", ch)], writes=["psN"], self_sync=False)
                    for hd in range(2):
                        r0, r1 = hd * 64, hd * 64 + 64
                        P.op("dve", lambda h, ch=ch, hd=hd, r0=r0, r1=r1: h.scalar_tensor_tensor(
                            out=Sf[r0:r1, p, :], in0=Sf[r0:r1, p, :], scalar=sm3[r0:r1, EBLp + ch:EBLp + ch + 1],
                            in1=psN[r0:r1, hd * 128:(hd + 1) * 128], op0=ALU.mult, op1=ALU.add),
                            reads=["Sf", ("ebl", p), "psN"], writes=["Sf"])
                    P.op("act", lambda h: h.activation(out=Sb[:, p, :], in_=Sf[:, p, :], func=AF.Copy), reads=["Sf"], writes=["Sb"])
                if last:
                    P.op("sp", lambda h: h.dma_start(out=So_d[l, p * 128:(p + 1) * 128, :], in_=Sf[:, p, :]),
                         reads=["Sf"], dma_sem=osem("Sf"))
                    cs, cl, ch = 128, 128, 1
                    for hd in range(2):
                        h4 = 2 * p + hd
                        qs = qsA if hd == 0 else qsB
                        qr = B(1) if hd == 0 else B(2)
                        ab = mcnt["att"] % 2
                        mcnt["att"] += 1
                        gb2 = mcnt["g"] % 2
                        mcnt["g"] += 1
                        attb, attr = ((mG[1], ("mG", 1)), (mV, "mV"))[gb2]
                        P.op("pe", lambda h, qs=qs, attb=attb: h.matmul(
                            attb[:, :128], lhsT=ks[:, 128:256], rhs=qs[:, 128:256], start=True, stop=True),
                            reads=[B(3), qr], writes=[attr], self_sync=False)
                        P.op("dve", lambda h, ab=ab, attb=attb: h.tensor_tensor(
                            out=attT[:, ab, :], in0=attb[:, :128], in1=smask, op=ALU.mult),
                            reads=[attr, "cst"], writes=[("attT", ab)])
                        P.op("pe", lambda h, ab=ab, hd=hd, h4=h4: h.matmul(
                            mO[hd][:, 128:256], lhsT=vtm[:, 1, h4 * 128:(h4 + 1) * 128], rhs=attT[:, ab, :],
                            start=True, stop=False, skip_group_check=True),
                            reads=[("vtm", 1), ("attT", ab)], writes=[("mO", hd)], self_sync=False)
                    for hf in range(2):
                        P.op("sp", lambda h, hf=hf: h.dma_start(
                            out=S0f[:].rearrange("p s v -> p (s v)"), in_=S0_d[l, p, hf]),
                            writes=["S0f"], dma_sem=osem("S0ld"))
                        P.op("act", lambda h: h.activation(out=S0b[:], in_=S0f[:], func=AF.Copy), reads=["S0f"], writes=["S0b"])
                        for sq_ in range(8):
                            sg_ = hf * 8 + sq_
                            for hd in range(2):
                                qs = qsA if hd == 0 else qsB
                                qr = B(1) if hd == 0 else B(2)
                                lastmm = (hf == 1 and sq_ == 7)
                                P.op("pe", lambda h, sq_=sq_, sg_=sg_, qs=qs, hd=hd, lastmm=lastmm: h.matmul(
                                    mO[hd][:, 128 + sg_ * 8:128 + sg_ * 8 + 8], lhsT=S0b[:, sq_, :],
                                    rhs=qs[:, 128 + sg_ * 8:128 + sg_ * 8 + 8], start=False, stop=lastmm, skip_group_check=True),
                                    reads=["S0b", qr], writes=[("mO", hd)], self_sync=False)
                    for hf in range(2):
                        P.op("sp", lambda h, hf=hf: h.dma_start(
                            out=S0f[:].rearrange("p s v -> p (s v)"), in_=S0_d[l, p, hf]),
                            writes=["S0f"], dma_sem=osem("S0ld"))
                        P.op("dve", lambda h, hf=hf: h.tensor_tensor(
                            out=kblk[:], in0=kendT[p][:, 1:2, :].to_broadcast([128, 8, 128]),
                            in1=blk16[:, hf * 8:(hf + 1) * 8].unsqueeze(2).to_broadcast([128, 8, 128]), op=ALU.mult),
                            reads=[("kendT", p), "cst"], writes=["S0b"])
                        for sq_ in range(8):
                            sg_ = hf * 8 + sq_
                            ub_, ur_ = ((psN, "psN"), (mV, "mV"))[sq_ % 2]
                            P.op("pe", lambda h, sq_=sq_, ub_=ub_: h.matmul(
                                ub_[:, 0:256], lhsT=kblk[:, sq_, :], rhs=vtm[:, 1, p * 256:(p + 1) * 256],
                                start=True, stop=True), reads=["S0b", ("vtm", 1)], writes=[ur_], self_sync=False)
                            for hd in range(2):
                                r0, r1 = hd * 64, hd * 64 + 64
                                P.op("dve", lambda h, sq_=sq_, sg_=sg_, hd=hd, r0=r0, r1=r1, ub_=ub_: h.scalar_tensor_tensor(
                                    out=S0f[r0:r1, sq_, :], in0=S0f[r0:r1, sq_, :],
                                    scalar=sm3[r0:r1, EBLp + 4 + sg_:EBLp + 5 + sg_],
                                    in1=ub_[r0:r1, hd * 128:(hd + 1) * 128], op0=ALU.mult, op1=ALU.add),
                                    reads=["S0f", ("ebl", p), ur_], writes=["S0f"])
                        P.op("sp", lambda h, hf=hf: h.dma_start(
                            out=Sos_d[l, p, hf], in_=S0f[:].rearrange("p s v -> p (s v)")), reads=["S0f"], dma_sem=osem("S0st"))
                for hd in range(2):
                    h4 = 2 * p + hd
                    osq, rst, sgo, t1 = Bb[5], Fb[9], Fb[10], Fb[7]
                    P.op("act", lambda h, hd=hd: h.activation(out=osq[:, :n], in_=mO[hd][:, :n], func=AF.Square),
                         reads=[("mO", hd)], writes=[B(5)])
                    P.op("pe", lambda h: h.matmul(psN[:, :n], lhsT=ones[:], rhs=osq[:, :n], start=True, stop=True),
                         reads=[B(5), "ones"], writes=["psN"], self_sync=False)
                    P.op("act", lambda h: h.activation(out=rst[:, :n], in_=psN[:, :n], func=AF.Sqrt, scale=1.0 / 128, bias=EPS),
                         reads=["psN"], writes=[F(9)], tbl="sqrt")
                    P.op("dve", lambda h: h.reciprocal(out=rst[:, :n], in_=rst[:, :n]), reads=[F(9)], writes=[F(9)])
                    goidx = [14, 17, 18, 19][h4]
                    psg, psgr = u_mm(goidx, ub=1)
                    P.op("act", lambda h, psg=psg: h.activation(out=sgo[:, :n], in_=psg[:, :n], func=AF.Tanh, scale=0.5),
                         reads=[psgr], writes=[F(10)], tbl="exp")
                    P.op("dve", lambda h, psg=psg: h.scalar_tensor_tensor(out=sgo[:, :n], in0=sgo[:, :n], scalar=1.0, in1=psg[:, :n],
                                                                         op0=ALU.add, op1=ALU.mult),
                         reads=[psgr, F(10)], writes=[F(10)])
                    P.op("dve", lambda h, hd=hd: h.scalar_tensor_tensor(out=t1[:, :n], in0=mO[hd][:, :n], scalar=0.5, in1=rst[:, :n],
                                                                       op0=ALU.mult, op1=ALU.mult),
                         reads=[("mO", hd), F(9)], writes=[F(7)])
                    P.op("dve", lambda h, h4=h4: h.scalar_tensor_tensor(
                        out=mixin[:, 4 + h4, :n], in0=t1[:, :n], scalar=cols[:, cb + 58:cb + 59], in1=sgo[:, :n],
                        op0=ALU.mult, op1=ALU.mult), reads=[F(7), F(10), "cols"], writes=[("mixin", 4 + h4)])

            for c_ in range(4):
                lru_chunk(c_)
            for p_ in range(2):
                gla_pair(p_)

            if last:
                released(mb + 7, extra_reads=["S0f"])
            for mc in range(KC):
                ub = mcnt["u"] % 2
                mcnt["u"] += 1
                for ic in range(KC):
                    sl_, pos = divmod(ic, 3)
                    rsl = (mb + 7 + sl_) % R
                    P.op("pe", lambda h, mc=mc, ic=ic, rsl=rsl, pos=pos, ub=ub: h.matmul(
                        mU[ub][:, :n], lhsT=ring[:, rsl, pos * 1024 + mc * 128:pos * 1024 + (mc + 1) * 128],
                        rhs=mixin[:, ic, :n], start=(ic == 0), stop=(ic == KC - 1)),
                        reads=[("slot", rsl), ("mixin", ic)], writes=[("mU", ub)], self_sync=False)
                P.op("dve", lambda h, mc=mc, ub=ub: h.tensor_tensor(out=x[:, mc, s:s + n], in0=mU[ub][:, :n],
                                                                     in1=x[:, mc, s:s + n], op=ALU.add),
                     reads=[("mU", ub), ("x", mc, ti)], writes=[("x", mc, ti)])


        n_issue = cfg.get("n_slots_used", nslots_total)
        issued = [0]

        def released(n_done, extra_reads=()):
            while issued[0] < min(n_issue, n_done + R):
                issue_slot(issued[0], extra_reads)
                issued[0] += 1

        released(0)

        if do_mixer:
            mixer_setup()
        else:
            P.seal(in_sem)
        obuf = scr[:, 13376:14400].rearrange("p (b n) -> p b n", b=2)

        def final_tile(ti):
            s, n = TILES[ti]
            if final_norm:
                for kc in range(KC):
                    sqa, sqr = ffn_sq[cnt["sq"] % 2]
                    cnt["sq"] += 1
                    P.op("act", lambda h, kc=kc, sqa=sqa: h.activation(out=sqa[:, :n], in_=x[:, kc, s:s + n], func=AF.Square),
                         reads=[("x", kc, ti)], writes=[sqr])
                    P.op("pe", lambda h, kc=kc, sqa=sqa: h.matmul(psN[:, :n], lhsT=ones[:], rhs=sqa[:, :n],
                                                                  start=(kc == 0), stop=(kc == KC - 1)),
                         reads=[sqr, "ones"], writes=["psN"], self_sync=False)
                rsa, rsr = ffn_rs[cnt["rs"] % 2]
                cnt["rs"] += 1
                P.op("act", lambda h: h.activation(out=rsa[:, :n], in_=psN[:, :n], func=AF.Sqrt, scale=1.0 / D, bias=EPS),
                     reads=["psN"], writes=[rsr])
                P.op("dve", lambda h: h.reciprocal(out=rsa[:, :n], in_=rsa[:, :n]), reads=[rsr], writes=[rsr])
                for kc in range(KC):
                    ob = cnt["ob"] % 2
                    cnt["ob"] += 1
                    P.op("dve", lambda h, kc=kc, ob=ob: h.scalar_tensor_tensor(
                        out=obuf[:, ob, :n], in0=x[:, kc, s:s + n], scalar=cols[:, 128 + kc:129 + kc],
                        in1=rsa[:, :n], op0=ALU.mult, op1=ALU.mult),
                        reads=[("x", kc, ti), rsr, "cols"], writes=[("ob", ob)])
                    P.op("sp", lambda h, kc=kc, ob=ob: h.dma_start(out=yT_d[kc * 128:(kc + 1) * 128, s:s + n], in_=obuf[:, ob, :n]),
                         reads=[("ob", ob)], dma_sem=ob_sem[ob])
            else:
                for kc in range(KC):
                    P.op("sp", lambda h, kc=kc: h.dma_start(out=yT_d[kc * 128:(kc + 1) * 128, s:s + n], in_=x[:, kc, s:s + n]),
                         reads=[("x", kc, ti)], dma_sem=ob_sem[kc % 2])

        final_done = False
        for l in range(L):
            base = l * PER_LAYER
            prenormed = (l > 0 and do_ffn2)
            if not do_mixer and not do_ffn2 and l == L - 1:
                ffn(l, 0, base, do_norm=not prenormed, after_tile=final_tile)
                final_done = True
                continue
            ffn(l, 0, base, do_norm=not prenormed)
            full_barrier()
            if do_mixer:
                mixer(l, base + NJ)
                full_barrier()
            if do_ffn2:
                if l + 1 < L:
                    nxt = (lambda ti, l=l: ffn_norm(l + 1, 0, ti))
                else:
                    nxt = final_tile
                    final_done = True
                ffn(l, 1, base + NJ + NMIX, do_norm=True, after_tile=nxt)
        if not final_done:
            full_barrier()
            for ti in range(len(TILES)):
                final_tile(ti)

        P.final_wait("sp", ob_sem + list(osems.values()))
        P.final_wait("pool", slot_sem)
        P.emit({"pe": block.tensor, "act": block.scalar, "dve": block.vector, "pool": block.gpsimd, "sp": block.sync})
    return nc


def _chunk_cols(w, c0, ncols=128):
    return w[:, c0:c0 + ncols].reshape(KC, 128, ncols).transpose(1, 0, 2)


def build_wslots(inp):
    ws = np.zeros((2 * PER_LAYER, 128, SLOT), np.float32)
    for l in range(2):
        base = l * PER_LAYER
        for f, (wgu, wdn) in enumerate([(inp["w_ffn1_gu"][l], inp["w_ffn1_down"][l]),
                                        (inp["w_ffn2_gu"][l], inp["w_ffn2_down"][l])]):
            b0 = base + (0 if f == 0 else NJ + NMIX)
            for j in range(NJ):
                ws[b0 + j, :, 0:1024] = _chunk_cols(wgu, j * 128).reshape(128, 1024)
                ws[b0 + j, :, 1024:2048] = _chunk_cols(wgu, DFF + j * 128).reshape(128, 1024)
                ws[b0 + j, :, 2048:3072] = wdn[j * 128:(j + 1) * 128, :]
        win = inp["w_in"][l]
        order = [0, 1, 2, 3, 4, 5, 6, 7, 8, 9, 10, 11, 12, 13, 16, 14, 15, 17, 18, 19]
        mb = base + NJ
        for idx, c in enumerate(order):
            sl, pos = divmod(idx, 3)
            ws[mb + sl, :, pos * 1024:(pos + 1) * 1024] = _chunk_cols(win, c * 128).reshape(128, 1024)
        ws[mb + 6, :, 2048:2048 + 128] = _chunk_cols(win, 2560, 16).reshape(128, 128)
        wo = inp["w_out"][l]
        for ic in range(8):
            sl, pos = divmod(ic, 3)
            ws[mb + 7 + sl, :, pos * 1024:(pos + 1) * 1024] = wo[ic * 128:(ic + 1) * 128, :]
    return ws


def build_cols(inp):
    c = np.zeros((128, NCOLS), np.float32)

    def put(col, vec, nch):
        c[:, col:col + nch] = vec.reshape(nch, 128).T

    for l in range(2):
        b = l * 64
        put(b + 0, inp["norm_ffn1"][l], 8)
        put(b + 8, inp["norm_mix"][l], 8)
        put(b + 16, inp["norm_ffn2"][l], 8)
        for k in range(4):
            put(b + 24 + k * 4, inp["lru_conv_w"][l][k], 4)
        put(b + 40, inp["lru_conv_b"][l], 4)
        put(b + 44, inp["lru_ba"][l], 4)
        put(b + 48, inp["lru_bx"][l], 4)
        put(b + 52, inp["lru_lambda"][l], 4)
        put(b + 56, inp["gla_b_gate"][l], 2)
        put(b + 58, inp["gla_norm"][l], 1)
    put(128, inp["norm_final"], 8)
    return c


def build_consts():
    c = np.zeros((128, NCST), np.float32)
    j = np.arange(128)
    c[:, 0:128] = np.eye(128, dtype=np.float32)
    c[:, 128:256] = (j[:, None] <= j[None, :]).astype(np.float32)
    c[:, 256:384] = ((j[:, None] <= j[None, :]) & (j[:, None] // 8 == j[None, :] // 8)).astype(np.float32)
    c[:, 384:400] = (j[:, None] // 8 == np.arange(16)[None, :]).astype(np.float32)
    c[:, 400:1040] = (np.arange(640) % 128 != 0).astype(np.float32)[None, :]
    c[:, 1040:1168] = (np.arange(128) % 8 != 0).astype(np.float32)[None, :]
    return c


_CFG = {}


def make_in_maps(inp):
    ws = build_wslots(inp)
    colsv = build_cols(inp)
    consts = build_consts()
    wab = np.ascontiguousarray(np.stack([inp["lru_wa"], inp["lru_wx"]], axis=1))
    metaT = inp["meta"].T
    in_maps = []
    for c in range(NCORES):
        xs = inp["x_sample"][16 * c:16 * c + 16].reshape(128, D)
        xT = np.ascontiguousarray(np.concatenate([metaT, inp["x_prompt"][c].T, xs.T], axis=1))
        sl = slice(16 * c, 16 * c + 16)
        in_maps.append({
            "xT": xT, "wslots": ws, "cols": colsv, "wab": wab, "consts": consts,
            "wg2": np.ascontiguousarray(inp["gla_w_gate2"]),
            "h0T": np.ascontiguousarray(inp["state_lru_h"][:, sl].transpose(0, 2, 1)),
            "c0T": np.ascontiguousarray(inp["state_lru_conv"][:, sl].transpose(0, 3, 1, 2)),
            "S0": np.ascontiguousarray(inp["state_gla_S"][:, sl].reshape(2, 2, 8, 2, 128, 128)
                                       .transpose(0, 3, 1, 4, 2, 5)).reshape(2, 2, 2, 128, 1024),
        })
    return in_maps


def kernel(**inputs):
    inp = {k: np.asarray(v) for k, v in inputs.items()}
    nc = build_program(dict(_CFG))
    in_maps = make_in_maps(inp)
    res = run_bass_kernel_spmd(nc, in_maps, core_ids=list(range(NCORES)))
    outs = res.results
    y_prompt = np.stack([outs[c]["yT"][:, 16:NPROMPT].T for c in range(NCORES)])
    y_sample = np.concatenate([outs[c]["yT"][:, NPROMPT:].T.reshape(16, 8, D) for c in range(NCORES)])
    hp = np.stack([outs[c]["hTo"][:, :, 0] for c in range(NCORES)], axis=1)
    hs = np.concatenate([outs[c]["hTo"][:, :, 1:].transpose(0, 2, 1) for c in range(NCORES)], axis=1)
    cp = np.stack([outs[c]["cTo"][:, :, 0, :].transpose(0, 2, 1) for c in range(NCORES)], axis=1)
    cs = np.concatenate([outs[c]["cTo"][:, :, 1:, :].transpose(0, 2, 3, 1) for c in range(NCORES)], axis=1)
    Sp = np.stack([outs[c]["So"].reshape(2, 4, 64, 128) for c in range(NCORES)], axis=1)
    Ss = np.concatenate([outs[c]["Sos"].reshape(2, 2, 2, 128, 8, 128).transpose(0, 2, 4, 1, 3, 5).reshape(2, 16, 4, 64, 128)
                         for c in range(NCORES)], axis=1)
    f = lambda a: np.ascontiguousarray(a, dtype=np.float32)
    return (f(y_prompt), f(y_sample), f(hp), f(cp), f(Sp), f(hs), f(cs), f(Ss))
```

```python
import contextlib
import numpy as np
import concourse.bass as bass
import concourse.mybir as mybir
from concourse.bass_utils import run_bass_kernel_spmd

F32 = mybir.dt.float32
BF16 = mybir.dt.bfloat16
AF = mybir.ActivationFunctionType
ALU = mybir.AluOpType

NCORES = 8
D = 1024
KC = 8
T = 2192
NPROMPT = 2064
TILES = [(0, 400), (400, 512), (912, 512), (1424, 512), (1936, 256)]
DFF = 2816
NJ = 22
GROUPS = [[0, 1, 2, 3], [4, 5, 6, 7], [8, 9, 10, 11], [12, 13, 14, 15], [16, 17, 18], [19, 20, 21]]
R = 11
SLOT = 3072
NMIX = 10
PER_LAYER = NJ + NMIX + NJ
EPS = 1e-6
NCOLS = 136
NCST = 1168
FW = 516


class Eng:
    def __init__(self, name, sem):
        self.name = name
        self.sem = sem
        self.count = 0
        self.seen = {}


class Prog:
    def __init__(self, sems, same_engine_sync=True):
        self.sems = list(sems)
        self.same = same_engine_sync
        self.engs = {}
        self.last_write = {}
        self.readers = {}
        self.dma_cnt = {}
        self.streams = {}
        self.divert = None
        self.raw_only = False
        self.sched_cost = None
        self.sched_mode = 0
        self.tbl_pen = 1300.0

    def add_engine(self, name):
        e = Eng(name, self.sems.pop())
        self.engs[name] = e
        self.streams[name] = []
        return e

    def new_sem(self):
        s = self.sems.pop()
        self.dma_cnt[id(s)] = 0
        return s

    def _deps(self, reads, writes, own_sem=None):
        toks = {}

        def add(t):
            k = id(t[0])
            if k not in toks or toks[k][1] < t[1]:
                toks[k] = t

        for r in reads:
            t = self.last_write.get(r)
            if t is not None:
                add(t)
        for w in writes:
            t = self.last_write.get(w)
            if t is not None and not (self.raw_only and t[0] is own_sem):
                add(t)
            for t in self.readers.get(w, ()):
                if not (self.raw_only and t[0] is own_sem):
                    add(t)
        return list(toks.values())

    def op(self, eng, fn, reads=(), writes=(), dma_sem=None, self_sync=None, grp=None, tbl=None):
        if self.divert is not None:
            self.divert.append((eng, fn, tuple(reads), tuple(writes), dma_sem, self_sync, tbl))
            return None
        e = self.engs[eng]
        same = self.same if self_sync is None else self_sync
        waits = []
        for sem, val in self._deps(reads, writes, e.sem):
            if sem is e.sem and not same:
                continue
            if e.seen.get(id(sem), 0) >= val:
                continue
            e.seen[id(sem)] = val
            waits.append((sem, val))
        if dma_sem is not None:
            self.dma_cnt[id(dma_sem)] += 16
            tok = (dma_sem, self.dma_cnt[id(dma_sem)])
            inc = (dma_sem, 16)
        else:
            e.count += 1
            tok = (e.sem, e.count)
            inc = (e.sem, 1)
        self.streams[eng].append((waits, fn, inc))
        for r in reads:
            self.readers.setdefault(r, []).append(tok)
        for w in writes:
            self.last_write[w] = tok
            self.readers[w] = []
        return tok

    def wait_tok(self, eng, tok):
        e = self.engs[eng]
        sem, val = tok
        if val == 0 or e.seen.get(id(sem), 0) >= val:
            return
        e.seen[id(sem)] = val
        self.streams[eng].append(([(sem, val)], None, None))

    def barrier(self, names=None, dma_sems=()):
        names = names or list(self.engs)
        for n in names:
            e = self.engs[n]
            for m in names:
                o = self.engs[m]
                if o is e or o.count == 0:
                    continue
                self.wait_tok(n, (o.sem, o.count))
            for ds in dma_sems:
                self.wait_tok(n, (ds, self.dma_cnt[id(ds)]))

    def merge(self, lists):
        pos = [0] * len(lists)
        tot = [max(len(l), 1) for l in lists]
        while True:
            best = None
            for i, l in enumerate(lists):
                if pos[i] < len(l):
                    fr = pos[i] / tot[i]
                    if best is None or fr < best[0]:
                        best = (fr, i)
            if best is None:
                break
            i = best[1]
            eng, fn, reads, writes, dma_sem, self_sync, grp = lists[i][pos[i]]
            pos[i] += 1
            self.op(eng, fn, reads=reads, writes=writes, dma_sem=dma_sem, self_sync=self_sync)

    def merge_sched(self, lists, cost=None):
        cost = cost or {"pe": 350.0, "act": 600.0, "dve": 600.0, "pool": 800.0, "sp": 2000.0}
        pos = [0] * len(lists)
        eng_free = {}
        wr_done = {}
        rd_done = {}

        def emit_one(i):
            eng, fn, reads, writes, dma_sem, self_sync, grp = lists[i][pos[i]]
            pos[i] += 1
            st = eng_free.get(eng, 0.0)
            for r in reads:
                st = max(st, wr_done.get(r, 0.0))
            for w in writes:
                st = max(st, wr_done.get(w, 0.0), rd_done.get(w, 0.0))
            fin = st + cost.get(eng, 500.0)
            eng_free[eng] = st + (60.0 if eng == "sp" else cost.get(eng, 500.0))
            for r in reads:
                rd_done[r] = max(rd_done.get(r, 0.0), fin)
            for w in writes:
                wr_done[w] = fin
                rd_done[w] = 0.0
            self.op(eng, fn, reads=reads, writes=writes, dma_sem=dma_sem, self_sync=self_sync)
            return grp

        while True:
            best = None
            for i, l in enumerate(lists):
                if pos[i] >= len(l):
                    continue
                eng, fn, reads, writes, dma_sem, self_sync, grp = l[pos[i]]
                st = eng_free.get(eng, 0.0)
                for r in reads:
                    st = max(st, wr_done.get(r, 0.0))
                for w in writes:
                    st = max(st, wr_done.get(w, 0.0), rd_done.get(w, 0.0))
                key = (st, -(len(l) - pos[i]))
                if best is None or key < best[0]:
                    best = (key, i)
            if best is None:
                break
            i = best[1]
            g = emit_one(i)
            while g is not None and pos[i] < len(lists[i]) and lists[i][pos[i]][6] == g:
                emit_one(i)

    def schedule(self, ops, cost=None):
        cost = cost or self.sched_cost or {"pe": 200.0, "act": 650.0, "dve": 650.0, "pool": 150.0, "sp": 2000.0}
        n = len(ops)
        preds = [set() for _ in range(n)]
        last_w, rdrs = {}, {}
        for i, (eng, fn, reads, writes, dma_sem, self_sync, grp) in enumerate(ops):
            for r in reads:
                if r in last_w:
                    preds[i].add(last_w[r])
            for w in writes:
                if w in last_w:
                    preds[i].add(last_w[w])
                preds[i].update(rdrs.get(w, ()))
            for r in reads:
                rdrs.setdefault(r, []).append(i)
            for w in writes:
                last_w[w] = i
                rdrs[w] = []
            preds[i].discard(i)
        succs = [[] for _ in range(n)]
        indeg = [0] * n
        for i in range(n):
            indeg[i] = len(preds[i])
            for p in preds[i]:
                succs[p].append(i)
        dur = [cost.get(op[0], 500.0) for op in ops]
        cp = [0.0] * n
        for i in range(n - 1, -1, -1):
            m = 0.0
            for sx in succs[i]:
                if cp[sx] > m:
                    m = cp[sx]
            cp[i] = dur[i] + m
        fin = [0.0] * n
        eng_free = {}
        cur_tbl = [None]
        ready = [i for i in range(n) if indeg[i] == 0]
        emitted = 0
        while ready:
            best = None
            for i in ready:
                eng = ops[i][0]
                st = eng_free.get(eng, 0.0)
                for p in preds[i]:
                    if fin[p] > st:
                        st = fin[p]
                if eng == "act" and ops[i][6] is not None and ops[i][6] != cur_tbl[0]:
                    st += self.tbl_pen
                if self.sched_mode == 1:
                    key = (st - 0.25 * cp[i], i)
                elif self.sched_mode == 2:
                    key = (-cp[i], st, i)
                else:
                    key = (st, -cp[i], i)
                if best is None or key < best[0]:
                    best = (key, i, st)
            _, i, st = best
            ready.remove(i)
            eng, fn, reads, writes, dma_sem, self_sync, tbl = ops[i]
            d_ = dur[i]
            if eng == "act" and tbl is not None and tbl != cur_tbl[0]:
                d_ += self.tbl_pen
                cur_tbl[0] = tbl
            fin[i] = st + d_
            eng_free[eng] = st + (60.0 if eng in ("sp", "pool") else d_)
            self.op(eng, fn, reads=reads, writes=writes, dma_sem=dma_sem, self_sync=self_sync)
            emitted += 1
            for sx in succs[i]:
                indeg[sx] -= 1
                if indeg[sx] == 0:
                    ready.append(sx)
        assert emitted == n, (emitted, n)

    def seal(self, sem):
        tot = self.dma_cnt[id(sem)]
        for k, t in list(self.last_write.items()):
            if t[0] is sem:
                self.last_write[k] = (sem, tot)

    def final_wait(self, eng, sems):
        for s in sems:
            self.wait_tok(eng, (s, self.dma_cnt[id(s)]))

    def emit(self, block_map):
        for name, deco in block_map.items():
            stream = self.streams[name]

            def body(h, stream=stream):
                for waits, fn, inc in stream:
                    for sem, val in waits:
                        h.wait_ge(sem, val)
                    if fn is not None:
                        fn(h).then_inc(inc[0], inc[1])

            deco(body)


def build_program(cfg):
    L = cfg.get("layers", 2)
    do_mixer = cfg.get("mixer", True)
    do_ffn2 = cfg.get("ffn2", True)
    final_norm = cfg.get("final_norm", True)
    nslots_total = 2 * PER_LAYER

    nc = bass.Bass("TRN2", target_bir_lowering=False)
    xT_d = nc.dram_tensor("xT", [D, T], F32, kind="ExternalInput").ap()
    w_d = nc.dram_tensor("wslots", [nslots_total, 128, SLOT], F32, kind="ExternalInput").ap()
    cols_d = nc.dram_tensor("cols", [128, NCOLS], F32, kind="ExternalInput").ap()
    wab_d = nc.dram_tensor("wab", [2, 2, 8, 64, 64], F32, kind="ExternalInput").ap()
    wg2_d = nc.dram_tensor("wg2", [2, 16, 256], F32, kind="ExternalInput").ap()
    h0T_d = nc.dram_tensor("h0T", [2, 512, 16], F32, kind="ExternalInput").ap()
    c0T_d = nc.dram_tensor("c0T", [2, 512, 16, 3], F32, kind="ExternalInput").ap()
    S0_d = nc.dram_tensor("S0", [2, 2, 2, 128, 1024], F32, kind="ExternalInput").ap()
    cst_d = nc.dram_tensor("consts", [128, NCST], F32, kind="ExternalInput").ap()
    yT_d = nc.dram_tensor("yT", [D, T], F32, kind="ExternalOutput").ap()
    hTo_d = nc.dram_tensor("hTo", [2, 512, 17], F32, kind="ExternalOutput").ap()
    cTo_d = nc.dram_tensor("cTo", [2, 512, 17, 3], F32, kind="ExternalOutput").ap()
    So_d = nc.dram_tensor("So", [2, 256, 128], F32, kind="ExternalOutput").ap()
    Sos_d = nc.dram_tensor("Sos", [2, 2, 2, 128, 1024], F32, kind="ExternalOutput").ap()

    es = contextlib.ExitStack()
    with es:
        x = es.enter_context(nc.sbuf_tensor("x", [128, KC, T], F32))
        NSCR = 15680
        scr = es.enter_context(nc.sbuf_tensor("scr", [128, NSCR], F32))
        ring = es.enter_context(nc.sbuf_tensor("ring", [128, R, SLOT], BF16))
        cols = es.enter_context(nc.sbuf_tensor("cols_sb", [128, NCOLS], F32))
        ones = es.enter_context(nc.sbuf_tensor("ones", [128, 128], BF16))
        psum = [es.enter_context(nc.psum_tensor(f"ps{i}", [128, 512], F32)) for i in range(8)]
        cst = es.enter_context(nc.sbuf_tensor("cst", [128, NCST], BF16))
        bd = es.enter_context(nc.sbuf_tensor("bd", [128, 16, 128], BF16))
        wg2b = es.enter_context(nc.sbuf_tensor("wg2b", [16, 2, 256], BF16))
        sm = es.enter_context(nc.sbuf_tensor("sm", [128, 160], F32))
        sm2 = es.enter_context(nc.sbuf_tensor("sm2", [128, 32], F32))
        sm3 = es.enter_context(nc.sbuf_tensor("sm3", [128, 96], F32))
        sems = [es.enter_context(nc.semaphore(f"s{i}")) for i in range(52)]
        P = Prog(sems, same_engine_sync=cfg.get("same_sync", True))
        P.raw_only = cfg.get("raw_only", False)
        P.sched_cost = cfg.get("sched_cost")
        P.sched_mode = cfg.get("sched_mode", 0)
        P.tbl_pen = cfg.get("tbl_pen", 1300.0)
        for n in ("pe", "act", "dve", "pool", "sp"):
            P.add_engine(n)
        slot_sem = [P.new_sem() for _ in range(R)]
        in_sem = P.new_sem()
        pin_sem = P.new_sem()
        ob_sem = [P.new_sem(), P.new_sem()]
        osems = {}

        def osem(name):
            if name not in osems:
                osems[name] = P.new_sem()
            return osems[name]
        block = es.enter_context(nc.Block())

        xn = scr[:, 0:4 * T].bitcast(BF16).rearrange("p (k t) -> p k t", k=KC)
        o0 = 4 * T
        hbuf = scr[:, o0:o0 + 2048].bitcast(BF16).rearrange("p (b j n) -> p b j n", b=2, j=4)
        sg = scr[:, o0 + 2048:o0 + 3072].rearrange("p (b n) -> p b n", b=2)
        sq = scr[:, o0 + 3072:o0 + 3584].bitcast(BF16).rearrange("p (b n) -> p b n", b=2)
        rs = scr[:, o0 + 3584:o0 + 4608].rearrange("p (b n) -> p b n", b=2)

        psG = [psum[0], psum[1]]
        psU = [psum[2], psum[3]]
        psD = [psum[4], psum[5], psum[6]]
        psN = psum[7]
        cnt = {"sq": 0, "rs": 0, "g": 0, "u": 0, "d": 0, "sg": 0, "unit": 0, "ob": 0}

        P.op("sp", lambda h: h.dma_start(out=cols[:], in_=cols_d), writes=["cols"], dma_sem=in_sem)
        x_sems = [P.new_sem() for _ in TILES]
        for t, (s_, n_) in enumerate(TILES):
            for kc in range(KC):
                P.op("sp", lambda h, kc=kc, s_=s_, n_=n_: h.dma_start(out=x[:, kc, s_:s_ + n_],
                                                                     in_=xT_d[kc * 128:(kc + 1) * 128, s_:s_ + n_]),
                     writes=[("x", kc, t)], dma_sem=x_sems[t])
            P.seal(x_sems[t])
        P.op("dve", lambda h: h.memset(ones[:], 1.0), writes=["ones"])

        def issue_slot(i, extra_reads=()):
            s = i % R
            P.op("pool",
                 lambda h, i=i, s=s: h.dma_start(out=ring[:, s, :].rearrange("p (a b) -> p a b", a=2),
                                                 in_=w_d[i].rearrange("p (a b) -> p a b", a=2)),
                 reads=list(extra_reads), writes=[("slot", s)], dma_sem=slot_sem[s])

        def full_barrier():
            P.barrier(["pe", "act", "dve", "sp"], dma_sems=ob_sem + list(osems.values()) + [in_sem, pin_sem])

        ffn_sq = [(sq[:, 0, :], ("sq", 0)), (sq[:, 1, :], ("sq", 1))]
        ffn_rs = [(rs[:, 0, :], ("rs", 0)), (rs[:, 1, :], ("rs", 1))]

        def norm_tile(ti, gcol, dst_fn, dst_res, sqb=None, rsb=None, lnexp=False):
            sqb = sqb or ffn_sq
            rsb = rsb or ffn_rs
            s, n = TILES[ti]
            for kc in range(KC):
                sqa, sqr = sqb[cnt["sq"] % 2]
                cnt["sq"] += 1
                P.op("act", lambda h, kc=kc, sqa=sqa: h.activation(out=sqa[:, :n], in_=x[:, kc, s:s + n], func=AF.Square),
                     reads=[("x", kc, ti)], writes=[sqr])
                P.op("pe", lambda h, kc=kc, sqa=sqa: h.matmul(psN[:, :n], lhsT=ones[:], rhs=sqa[:, :n],
                                                              start=(kc == 0), stop=(kc == KC - 1)),
                     reads=[sqr, "ones"], writes=["psN"], self_sync=False)
            rsa, rsr = rsb[cnt["rs"] % 2]
            cnt["rs"] += 1
            if lnexp:
                P.op("act", lambda h: h.activation(out=rsa[:, :n], in_=psN[:, :n], func=AF.Ln, scale=1.0 / D, bias=EPS),
                     reads=["psN"], writes=[rsr], tbl="ln")
                P.op("act", lambda h: h.activation(out=rsa[:, :n], in_=rsa[:, :n], func=AF.Exp, scale=-0.5),
                     reads=[rsr], writes=[rsr], tbl="exp")
            else:
                P.op("act", lambda h: h.activation(out=rsa[:, :n], in_=psN[:, :n], func=AF.Sqrt, scale=1.0 / D, bias=EPS),
                     reads=["psN"], writes=[rsr], tbl="sqrt")
                P.op("dve", lambda h: h.reciprocal(out=rsa[:, :n], in_=rsa[:, :n]),
                     reads=[rsr], writes=[rsr])
            for kc in range(KC):
                P.op("dve", lambda h, kc=kc: h.scalar_tensor_tensor(
                    out=dst_fn(kc, s, n), in0=x[:, kc, s:s + n], scalar=cols[:, gcol + kc:gcol + kc + 1],
                    in1=rsa[:, :n], op0=ALU.mult, op1=ALU.mult),
                    reads=[("x", kc, ti), rsr, "cols"], writes=[dst_res(kc, ti)])

        def ffn_norm(l, f, ti):
            gcol = l * 64 + (0 if f == 0 else 16)
            norm_tile(ti, gcol, lambda kc, s, n: xn[:, kc, s:s + n], lambda kc, ti_: ("xn", kc, ti_))

        def ffn(l, f, slot_base, do_norm=True, after_tile=None):
            units = [(g, ti) for g in range(len(GROUPS)) for ti in range(len(TILES))]

            def phaseA(u):
                g, ti = units[u]
                s, n = TILES[ti]
                hb = u % 2
                for jj, j in enumerate(GROUPS[g]):
                    sl = (slot_base + j) % R
                    gb = cnt["g"] % 2
                    cnt["g"] += 1
                    for kc in range(KC):
                        P.op("pe", lambda h, kc=kc, sl=sl, gb=gb: h.matmul(
                            psG[gb][:, :n], lhsT=ring[:, sl, kc * 128:(kc + 1) * 128], rhs=xn[:, kc, s:s + n],
                            start=(kc == 0), stop=(kc == KC - 1)),
                            reads=[("slot", sl), ("xn", kc, ti)], writes=[("psG", gb)], self_sync=False)
                    for kc in range(KC):
                        P.op("pe", lambda h, kc=kc, sl=sl, gb=gb: h.matmul(
                            psU[gb][:, :n], lhsT=ring[:, sl, 1024 + kc * 128:1024 + (kc + 1) * 128],
                            rhs=xn[:, kc, s:s + n], start=(kc == 0), stop=(kc == KC - 1)),
                            reads=[("slot", sl), ("xn", kc, ti)], writes=[("psU", gb)], self_sync=False)
                    sb = cnt["sg"] % 2
                    cnt["sg"] += 1
                    P.op("act", lambda h, gb=gb, sb=sb: h.activation(out=sg[:, sb, :n], in_=psG[gb][:, :n], func=AF.Silu),
                         reads=[("psG", gb)], writes=[("sg", sb)])
                    P.op("dve", lambda h, gb=gb, sb=sb, jj=jj: h.tensor_tensor(
                        out=hbuf[:, hb, jj, :n], in0=psU[gb][:, :n], in1=sg[:, sb, :n], op=ALU.mult),
                        reads=[("psU", gb), ("sg", sb)], writes=[("h", hb, jj)])

            def phaseB(u):
                g, ti = units[u]
                s, n = TILES[ti]
                hb = u % 2
                ng = len(GROUPS[g])
                for mc in range(KC):
                    db = cnt["d"] % 3
                    cnt["d"] += 1
                    for jj, j in enumerate(GROUPS[g]):
                        sl = (slot_base + j) % R
                        P.op("pe", lambda h, mc=mc, sl=sl, db=db, jj=jj: h.matmul(
                            psD[db][:, :n], lhsT=ring[:, sl, 2048 + mc * 128:2048 + (mc + 1) * 128],
                            rhs=hbuf[:, hb, jj, :n], start=(jj == 0), stop=(jj == ng - 1)),
                            reads=[("slot", sl), ("h", hb, jj)], writes=[("psD", db)], self_sync=False)
                    P.op("dve", lambda h, mc=mc, db=db: h.scalar_tensor_tensor(
                        out=x[:, mc, s:s + n], in0=psD[db][:, :n], scalar=0.5, in1=x[:, mc, s:s + n],
                        op0=ALU.mult, op1=ALU.add),
                        reads=[("psD", db), ("x", mc, ti)], writes=[("x", mc, ti)])

            NT_ = len(TILES)
            NA_ = cfg.get("norm_ahead", 2)
            if do_norm:
                for t_ in range(min(NA_ + 1, NT_)):
                    ffn_norm(l, f, t_)
            phaseA(0)
            pending = []
            for u in range(len(units)):
                if u + 1 < len(units):
                    g1, t1 = units[u + 1]
                    lst = []
                    for pl in pending:
                        lst.extend(pl)
                    pending = []
                    P.divert = lst
                    if do_norm and g1 == 0 and t1 + NA_ < NT_:
                        ffn_norm(l, f, t1 + NA_)
                    phaseA(u + 1)
                    P.divert = None
                    if cfg.get("ffn_sched", True):
                        P.schedule(lst)
                    else:
                        P.merge([lst])
                phaseB(u)
                g, ti = units[u]
                if ti == len(TILES) - 1:
                    released(slot_base + GROUPS[g][-1] + 1)
                if g == len(GROUPS) - 1 and after_tile is not None:
                    if u + 2 < len(units):
                        tmp = []
                        P.divert = tmp
                        after_tile(ti)
                        P.divert = None
                        pending.append(tmp)
                    else:
                        for pl in pending:
                            P.merge([pl])
                        pending = []
                        after_tile(ti)

        ident = cst[:, 0:128]
        maskT = cst[:, 128:256]
        smask = cst[:, 256:384]
        blk16 = cst[:, 384:400]
        Mmask = cst[:, 400:1040]
        m8 = cst[:, 1040:1168]
        xnt2 = [scr[:, 0:2048].bitcast(BF16).rearrange("p (k n) -> p k n", k=KC)]
        mixin = scr[:, 2048:4096].bitcast(BF16).rearrange("p (k n) -> p k n", k=KC)
        Fb = [scr[:, 4096 + i * FW:4096 + (i + 1) * FW] for i in range(12)]
        ob_ = 4096 + 12 * FW
        Bb = [scr[:, ob_ + i * 256:ob_ + (i + 1) * 256].bitcast(BF16) for i in range(6)]
        ob_ += 6 * 256
        vtm = scr[:, ob_:ob_ + 1024].bitcast(BF16).rearrange("p (c v) -> p c v", c=4)
        ob_ += 1024
        kendT = [scr[:, ob_ + i * 256:ob_ + (i + 1) * 256].bitcast(BF16).rearrange("p (c v) -> p c v", c=4) for i in range(2)]
        ob_ += 512
        attT = scr[:, ob_:ob_ + 128].bitcast(BF16).rearrange("p (b v) -> p b v", b=2)
        ob_ += 128
        Sf = scr[:, ob_:ob_ + 256].rearrange("p (b v) -> p b v", b=2)
        ob_ += 256
        Sb = scr[:, ob_:ob_ + 128].bitcast(BF16).rearrange("p (b v) -> p b v", b=2)
        ob_ += 128
        lrT = scr[:, ob_:ob_ + 256].bitcast(BF16)
        ob_ += 256
        S0f = scr[:, ob_:ob_ + 1024].rearrange("p (s v) -> p s v", s=8)
        ob_ += 1024
        S0b = scr[:, ob_:ob_ + 512].bitcast(BF16).rearrange("p (s v) -> p s v", s=8)
        ob_ += 512
        kblk = S0b
        assert ob_ <= NSCR, ob_
        F = lambda i: ("F", i)
        B = lambda i: ("B", i)
        C1, C2, SPT, HIST, HPREV, BLC, EBL, TMP16 = 0, 8, 16, 64, 76, 80, 100, 120
        mU = [psum[0], psum[1]]
        mG = [psum[2], psum[3]]
        mO = [psum[4], psum[5]]
        mV = psum[6]
        mVb = psum[6][:, :].bitcast(BF16)
        mcnt = {"u": 0, "g": 0, "att": 0}

        def mixer_setup():
            P.op("pool", lambda h: h.dma_start(out=cst[:], in_=cst_d), writes=["cst"], dma_sem=pin_sem)
            P.op("dve", lambda h: h.memset(bd[:], 0.0), writes=["bd"])
            for l in range(2):
                for wh in range(2):
                    for c in range(4):
                        for half in range(2):
                            P.op("pool", lambda h, l=l, wh=wh, c=c, half=half: h.dma_start(
                                out=bd[half * 64:(half + 1) * 64, l * 8 + wh * 4 + c, half * 64:(half + 1) * 64],
                                in_=wab_d[l, wh, 2 * c + half]), reads=["bd"], writes=[("bdp", l, wh, c, half)], dma_sem=pin_sem)
                P.op("pool", lambda h, l=l: h.dma_start(out=wg2b[:, l, :], in_=wg2_d[l]), writes=[("wg2bp", l)], dma_sem=pin_sem)
            P.seal(in_sem)
            P.seal(pin_sem)
            P.last_write["bd"] = (pin_sem, P.dma_cnt[id(pin_sem)])
            P.readers["bd"] = []
            P.last_write["wg2b"] = (pin_sem, P.dma_cnt[id(pin_sem)])
            e_ = sm[:, SPT:SPT + 8]
            ser = sm[:, SPT + 8:SPT + 16]
            lnv = sm[:, SPT + 16:SPT + 24]
            msk = sm[:, SPT + 24:SPT + 32]
            for l in range(2):
                P.op("act", lambda h, l=l: h.activation(out=sm[:, SPT + l * 4:SPT + l * 4 + 4], in_=cols[:, l * 64 + 52:l * 64 + 56],
                                                        func=AF.Exp, scale=-1.0), reads=["cols"], writes=["spt"], tbl="exp")
            P.op("act", lambda h: h.activation(out=lnv, in_=e_, func=AF.Ln, bias=1.0, scale=1.0), reads=["spt"], writes=["spt"])
            P.op("dve", lambda h: h.tensor_scalar(out=ser, in0=e_, scalar1=-1.0 / 6, scalar2=1.0 / 5, op0=ALU.mult, op1=ALU.add),
                 reads=["spt"], writes=["spt"])
            for cf in (1.0 / 4, 1.0 / 3, 1.0 / 2, 1.0):
                P.op("dve", lambda h: h.tensor_tensor(out=ser, in0=ser, in1=e_, op=ALU.mult), reads=["spt"], writes=["spt"])
                P.op("dve", lambda h, cf=cf: h.tensor_scalar(out=ser, in0=ser, scalar1=-1.0, scalar2=cf, op0=ALU.mult, op1=ALU.add),
                     reads=["spt"], writes=["spt"])
            P.op("dve", lambda h: h.tensor_tensor(out=ser, in0=ser, in1=e_, op=ALU.mult), reads=["spt"], writes=["spt"])
            P.op("dve", lambda h: h.tensor_single_scalar(out=msk, in_=e_, scalar=0.25, op=ALU.is_lt), reads=["spt"], writes=["spt"])
            P.op("dve", lambda h: h.tensor_tensor(out=ser, in0=ser, in1=lnv, op=ALU.subtract), reads=["spt"], writes=["spt"])
            P.op("dve", lambda h: h.tensor_tensor(out=ser, in0=ser, in1=msk, op=ALU.mult), reads=["spt"], writes=["spt"])
            P.op("dve", lambda h: h.tensor_tensor(out=ser, in0=ser, in1=lnv, op=ALU.add), reads=["spt"], writes=["spt"])
            P.op("dve", lambda h: h.tensor_scalar(out=sm[:, C1:C1 + 8], in0=ser, scalar1=-8.0, scalar2=None, op0=ALU.mult),
                 reads=["spt"], writes=["dcol"])
            P.op("dve", lambda h: h.tensor_scalar(out=sm[:, C2:C2 + 8], in0=ser, scalar1=-16.0, scalar2=None, op0=ALU.mult),
                 reads=["spt"], writes=["dcol"])
            for l in range(2):
                P.op("dve", lambda h, l=l: h.tensor_scalar(out=sm2[:, l * 4:l * 4 + 4], in0=cols[:, l * 64 + 44:l * 64 + 48],
                                                           scalar1=0.5, scalar2=None, op0=ALU.mult), reads=["cols"], writes=["sm2"])
                P.op("dve", lambda h, l=l: h.tensor_scalar(out=sm2[:, 8 + l * 4:8 + l * 4 + 4], in0=cols[:, l * 64 + 48:l * 64 + 52],
                                                           scalar1=0.5, scalar2=None, op0=ALU.mult), reads=["cols"], writes=["sm2"])
                P.op("dve", lambda h, l=l: h.tensor_scalar(out=sm2[:, 24 + l * 2:24 + l * 2 + 2], in0=cols[:, l * 64 + 56:l * 64 + 58],
                                                           scalar1=0.5, scalar2=None, op0=ALU.mult), reads=["cols"], writes=["sm2"])
            P.op("dve", lambda h: h.tensor_scalar(out=sm2[:, 16:24], in0=sm[:, C1:C1 + 8], scalar1=0.5, scalar2=None, op0=ALU.mult),
                 reads=["dcol"], writes=["sm2"])

        def mixer(l, mb):
            cb = l * 64

            def wsl(idx):
                sl_, pos = divmod(idx, 3)
                return (mb + sl_) % R, pos * 1024

            P.divert = []
            P.op("dve", lambda h: h.memset(sm[:, HIST:HIST + 16], 0.0), writes=["hist", "hprev"])
            P.op("dve", lambda h: h.memset(Sf[:], 0.0), writes=["Sf"])
            P.op("dve", lambda h: h.memset(Sb[:], 0.0), writes=["Sb"])
            P.op("dve", lambda h: h.memset(Bb[1][:], 0.0), writes=[B(1)])
            P.op("dve", lambda h: h.memset(Bb[2][:], 0.0), writes=[B(2)])

            for ti in range(len(TILES)):
                mixer_tile(l, mb, cb, wsl, ti, None)
            ops = P.divert
            P.divert = None
            nfill = cfg.get("fill", 0)
            for k_ in range(nfill):
                ops.append(("pe", (lambda h: h.matmul(psum[2][:, :512], lhsT=ones[:], rhs=cst[:, 400:912], start=True, stop=True)),
                            ("cst", "ones"), (("fill", k_),), None, False, None))
            if cfg.get("zip", True):
                P.schedule(ops)
            else:
                P.merge([ops])
            released(mb + NMIX)

        def mixer_tile(l, mb, cb, wsl, ti, pendW):
            s, n = TILES[ti]
            last = (ti == len(TILES) - 1)
            npr = 128 if last else n
            if ti == 0:
                chunks = [(0, 16), (16, 128), (144, 128), (272, 128)]
            elif last:
                chunks = [(0, 128)]
            else:
                chunks = [(i * 128, 128) for i in range(4)]
            cm = Mmask[:, 112:112 + n] if ti == 0 else Mmask[:, 0:npr]

            xpar = 0
            xnt = xnt2[0]
            norm_tile(ti, cb + 8, lambda kc, s_, n_: xnt[:, kc, :n_], lambda kc, ti_: ("xnt", xpar, kc),
                      sqb=[(Bb[4], B(4)), (Bb[5], B(5))], rsb=[(Fb[5], F(5)), (Fb[6], F(6))], lnexp=False)

            def u_mm(idx, m=128, lr=False, ub=0):
                rsl, off = wsl(idx)
                for kc in range(KC):
                    if lr:
                        lhs = ring[:, rsl, 2048 + kc * 16:2048 + kc * 16 + 16]
                    else:
                        lhs = ring[:, rsl, off + kc * 128:off + (kc + 1) * 128]
                    P.op("pe", lambda h, kc=kc, lhs=lhs: h.matmul(mU[ub][:m, :n], lhsT=lhs, rhs=xnt[:, kc, :n],
                                                                 start=(kc == 0), stop=(kc == KC - 1)),
                         reads=[("slot", rsl), ("xnt", xpar, kc)], writes=[("mU", ub)], self_sync=False)
                return mU[ub], ("mU", ub)

            def lru_chunk(c):
                xp, xps, xc, ra, it, am, hs = Fb[0], Fb[1], Fb[2], Fb[3], Fb[4], Fb[5], Fb[6]
                ps, psr = u_mm(c, ub=0)
                wcol = lambda k: cols[:, cb + 24 + k * 4 + c:cb + 25 + k * 4 + c]
                bcol = cols[:, cb + 40 + c:cb + 41 + c]
                P.op("dve", lambda h: h.tensor_copy(out=xp[:, 0:3], in_=sm[:, HIST + c * 3:HIST + c * 3 + 3]),
                     reads=["hist"], writes=[F(0)])
                P.op("act", lambda h: h.activation(out=xp[:, 3:3 + npr], in_=ps[:, 0:npr], func=AF.Copy),
                     reads=[psr], writes=[F(0)])
                if last:
                    xps3 = xps[:, 0:176].rearrange("p (s k) -> p s k", k=11)
                    P.op("act", lambda h: h.activation(out=xps3[:, :, 3:11], in_=ps[:, 128:256].rearrange("p (s k) -> p s k", k=8),
                                                       func=AF.Copy), reads=[psr], writes=[F(1)])
                    P.op("sp", lambda h: h.dma_start(out=xps3[:, :, 0:3], in_=c0T_d[l, c * 128:(c + 1) * 128, :, :]),
                         writes=[F(1)], dma_sem=osem("c0ld"))
                    P.op("sp", lambda h: h.dma_start(out=xps[:, 176:192], in_=h0T_d[l, c * 128:(c + 1) * 128, :]),
                         writes=[F(1)], dma_sem=osem("c0ld"))
                P.op("dve", lambda h: h.tensor_scalar(out=xc[:, 0:npr], in0=xp[:, 0:npr], scalar1=wcol(0), scalar2=bcol,
                                                      op0=ALU.mult, op1=ALU.add), reads=[F(0), "cols"], writes=[F(2)])
                for k in range(1, 4):
                    P.op("dve", lambda h, k=k: h.scalar_tensor_tensor(out=xc[:, 0:npr], in0=xp[:, k:k + npr], scalar=wcol(k),
                                                                      in1=xc[:, 0:npr], op0=ALU.mult, op1=ALU.add),
                         reads=[F(0), F(2), "cols"], writes=[F(2)])
                if last:
                    xcs = xc[:, 128:256].rearrange("p (s k) -> p s k", k=8)
                    P.op("dve", lambda h: h.tensor_scalar(out=xcs, in0=xps3[:, :, 0:8], scalar1=wcol(0), scalar2=bcol,
                                                          op0=ALU.mult, op1=ALU.add), reads=[F(1), "cols"], writes=[F(2)])
                    for k in range(1, 4):
                        P.op("dve", lambda h, k=k: h.scalar_tensor_tensor(out=xcs, in0=xps3[:, :, k:k + 8], scalar=wcol(k),
                                                                          in1=xcs, op0=ALU.mult, op1=ALU.add),
                             reads=[F(1), F(2), "cols"], writes=[F(2)])
                    P.op("sp", lambda h: h.dma_start(out=cTo_d[l, c * 128:(c + 1) * 128, 0, :], in_=xp[:, npr:npr + 3]),
                         reads=[F(0)], dma_sem=osem("F0"))
                    P.op("sp", lambda h: h.dma_start(out=cTo_d[l, c * 128:(c + 1) * 128, 1:17, :], in_=xps3[:, :, 8:11]),
                         reads=[F(1)], dma_sem=osem("F1"))
                else:
                    P.op("dve", lambda h: h.tensor_copy(out=sm[:, HIST + c * 3:HIST + c * 3 + 3], in_=xp[:, n:n + 3]),
                         reads=[F(0)], writes=["hist"])
                P.op("act", lambda h: h.activation(out=Bb[0][:, :n], in_=xc[:, :n], func=AF.Copy), reads=[F(2)], writes=[B(0)])
                P.op("pe", lambda h: h.matmul(mG[0][:, :n], lhsT=bd[:, l * 8 + 0 * 4 + c, :], rhs=Bb[0][:, :n], start=True, stop=True),
                     reads=[B(0), "bd"], writes=[("mG", 0)], self_sync=False)
                P.op("act", lambda h: h.activation(out=ra[:, :n], in_=mG[0][:, :n], func=AF.Tanh, scale=0.5,
                                                   bias=sm2[:, l * 4 + c:l * 4 + c + 1]), reads=[("mG", 0), "sm2"], writes=[F(3)], tbl="exp")
                P.op("pe", lambda h: h.matmul(mG[0][:, :n], lhsT=bd[:, l * 8 + 1 * 4 + c, :], rhs=Bb[0][:, :n], start=True, stop=True),
                     reads=[B(0), "bd"], writes=[("mG", 0)], self_sync=False)
                P.op("act", lambda h: h.activation(out=it[:, :n], in_=mG[0][:, :n], func=AF.Tanh, scale=0.5,
                                                   bias=sm2[:, 8 + l * 4 + c:8 + l * 4 + c + 1]), reads=[("mG", 0), "sm2"], writes=[F(4)], tbl="exp")
                P.op("act", lambda h: h.activation(out=am[:, :n], in_=ra[:, :n], func=AF.Exp,
                                                   scale=sm[:, C1 + l * 4 + c:C1 + l * 4 + c + 1],
                                                   bias=sm[:, C1 + l * 4 + c:C1 + l * 4 + c + 1]), reads=[F(3), "dcol"], writes=[F(5)], tbl="exp")
                P.op("act", lambda h: h.activation(out=ra[:, :n], in_=ra[:, :n], func=AF.Exp,
                                                   scale=sm2[:, 16 + l * 4 + c:16 + l * 4 + c + 1],
                                                   bias=sm2[:, 16 + l * 4 + c:16 + l * 4 + c + 1]), reads=[F(3), "sm2"], writes=[F(3)], tbl="exp")
                P.op("act", lambda h: h.activation(out=am[:, :n], in_=am[:, :n], func=AF.Sqrt, scale=-0.25, bias=0.25),
                     reads=[F(5)], writes=[F(5)], tbl="sqrt")
                P.op("dve", lambda h: h.scalar_tensor_tensor(out=it[:, :n], in0=it[:, :n], scalar=1.0, in1=xc[:, :n],
                                                             op0=ALU.add, op1=ALU.mult),
                     reads=[F(4), F(2)], writes=[F(4)])
                if ti == 0:
                    P.op("dve", lambda h: h.memset(am[:, 0:1], 0.5), reads=[], writes=[F(5)])
                P.op("dve", lambda h: h.tensor_tensor(out=it[:, :n], in0=it[:, :n], in1=am[:, :n], op=ALU.mult),
                     reads=[F(4), F(5)], writes=[F(4)])
                if last:
                    a_st = ra[:, 128:256].rearrange("p (s k) -> p s k", k=8)[:, :, 0]
                    b_st = it[:, 128:256].rearrange("p (s k) -> p s k", k=8)[:, :, 0]
                    t16 = sm[:, TMP16:TMP16 + 16]
                    P.op("dve", lambda h: h.tensor_tensor(out=t16, in0=a_st, in1=xps[:, 176:192], op=ALU.mult),
                         reads=[F(3), F(1)], writes=["t16"])
                    P.op("dve", lambda h: h.tensor_tensor(out=b_st, in0=b_st, in1=t16, op=ALU.add),
                         reads=[F(4), "t16"], writes=[F(4)])
                    P.op("dve", lambda h: h.memset(a_st, 0.0), reads=[], writes=[F(3)])
                init = 0.0 if ti == 0 else sm[:, HPREV + c:HPREV + c + 1]
                P.op("dve", lambda h: h.tensor_tensor_scan(out=hs[:, :n], data0=ra[:, :n], data1=it[:, :n], initial=init,
                                                           op0=ALU.mult, op1=ALU.add),
                     reads=[F(3), F(4), "hprev"], writes=[F(6)])
                if last:
                    P.op("dve", lambda h: h.tensor_copy(out=sm[:, 137:138], in_=hs[:, 127:128]), reads=[F(6)], writes=["hs16"])
                    P.op("dve", lambda h: h.tensor_copy(out=sm[:, 138:154],
                                                        in_=hs[:, 128:256].rearrange("p (s k) -> p s k", k=8)[:, :, 7]),
                         reads=[F(6)], writes=["hs16"])
                    P.op("sp", lambda h: h.dma_start(out=hTo_d[l, c * 128:(c + 1) * 128, :], in_=sm[:, 137:154]),
                         reads=["hs16"], dma_sem=osem("hs16"))
                else:
                    P.op("dve", lambda h: h.tensor_copy(out=sm[:, HPREV + c:HPREV + c + 1], in_=hs[:, n - 1:n]),
                         reads=[F(6)], writes=["hprev"])
                ps2, ps2r = u_mm(4 + c, ub=0)
                P.op("act", lambda h: h.activation(out=xp[:, :n], in_=ps2[:, :n], func=AF.Square), reads=[ps2r], writes=[F(0)])
                P.op("dve", lambda h: h.tensor_scalar(out=xp[:, :n], in0=xp[:, :n], scalar1=0.044715, scalar2=1.0,
                                                      op0=ALU.mult, op1=ALU.add), reads=[F(0)], writes=[F(0)])
                P.op("dve", lambda h: h.tensor_tensor(out=xp[:, :n], in0=ps2[:, :n], in1=xp[:, :n], op=ALU.mult),
                     reads=[F(0), ps2r], writes=[F(0)])
                P.op("act", lambda h: h.activation(out=xp[:, :n], in_=xp[:, :n], func=AF.Tanh, scale=0.7978845608028654),
                     reads=[F(0)], writes=[F(0)], tbl="exp")
                P.op("dve", lambda h: h.scalar_tensor_tensor(out=xp[:, :n], in0=xp[:, :n], scalar=1.0, in1=ps2[:, :n],
                                                             op0=ALU.add, op1=ALU.mult), reads=[F(0), ps2r], writes=[F(0)])
                P.op("dve", lambda h: h.scalar_tensor_tensor(out=mixin[:, c, :n], in0=xp[:, :n], scalar=0.5, in1=hs[:, :n],
                                                             op0=ALU.mult, op1=ALU.mult),
                     reads=[F(6), F(0)], writes=[("mixin", c)])


            psl, pslr = u_mm(18, m=16, lr=True, ub=1)
            P.op("act", lambda h: h.activation(out=lrT[0:16, :n], in_=psl[0:16, :n], func=AF.Copy), reads=[pslr], writes=["lrT"])
            vchunks = chunks + ([(128, 128)] if last else [])
            for ch, (cs, cl) in enumerate(vchunks):
                for pr in range(2):
                    rsl = (mb + 4 + pr) % R
                    rv = ring[:, rsl, :].rearrange("p (c k m) -> p c k m", c=3, k=KC)
                    for kc in range(KC):
                        P.op("pe", lambda h, kc=kc, pr=pr, rv=rv, cs=cs, cl=cl: h.matmul(
                            mV[:cl, pr * 256:(pr + 1) * 256], lhsT=xnt[:, kc, cs:cs + cl], rhs=rv[:, 0:2, kc, :],
                            start=(kc == 0), stop=(kc == KC - 1)),
                            reads=[("slot", rsl), ("xnt", xpar, kc)], writes=["mV"], self_sync=False)
                P.op("act", lambda h, ch=ch, cl=cl: h.activation(out=vtm[:cl, ch, :], in_=mV[:cl, :], func=AF.Copy),
                     reads=["mV"], writes=[("vtm", ch)])

            def gla_pair(p):
                BLCp, EBLp = p * 48, p * 48 + 24
                gg, b16, eb, enb, ek = Fb[7], Fb[8], Fb[9], Fb[10], Fb[11]
                qsA, qsB, ks, kend = Bb[1], Bb[2], Bb[3], Bb[4]
                P.op("pe", lambda h: h.matmul(mU[1][:, :n], lhsT=wg2b[:, l, p * 128:(p + 1) * 128], rhs=lrT[0:16, :n],
                                              start=True, stop=True), reads=["wg2b", "lrT"], writes=[("mU", 1)], self_sync=False)
                P.op("act", lambda h: h.activation(out=gg[:, :n], in_=mU[1][:, :n], func=AF.Tanh, scale=0.5,
                                                   bias=sm2[:, 24 + l * 2 + p:24 + l * 2 + p + 1]), reads=[("mU", 1), "sm2"], writes=[F(7)], tbl="exp")
                P.op("act", lambda h: h.activation(out=gg[:, :n], in_=gg[:, :n], func=AF.Ln, scale=0.5, bias=0.5),
                     reads=[F(7)], writes=[F(7)], tbl="ln")
                P.op("dve", lambda h: h.tensor_tensor_scan(out=b16[:, :npr], data0=cm, data1=gg[:, :npr], initial=0.0,
                                                           op0=ALU.mult, op1=ALU.add), reads=[F(7), "cst"], writes=[F(8)])
                if last:
                    P.op("dve", lambda h: h.tensor_tensor_scan(out=b16[:, 128:256], data0=m8, data1=gg[:, 128:256], initial=0.0,
                                                               op0=ALU.mult, op1=ALU.add), reads=[F(7), "cst"], writes=[F(8)])
                P.op("act", lambda h: h.activation(out=eb[:, :n], in_=b16[:, :n], func=AF.Exp, scale=1.0 / 16),
                     reads=[F(8)], writes=[F(9)], tbl="exp")
                P.op("act", lambda h: h.activation(out=enb[:, :n], in_=b16[:, :n], func=AF.Exp, scale=-1.0 / 16),
                     reads=[F(8)], writes=[F(10)], tbl="exp")
                for ch, (cs, cl) in enumerate(chunks):
                    P.op("dve", lambda h, ch=ch, cs=cs, cl=cl: h.tensor_scalar(
                        out=sm3[:, BLCp + ch:BLCp + ch + 1], in0=b16[:, cs + cl - 1:cs + cl], scalar1=1.0 / 16, scalar2=None,
                        op0=ALU.mult), reads=[F(8)], writes=[("blc", p)])
                    P.op("act", lambda h, ch=ch, cs=cs, cl=cl: h.activation(
                        out=ek[:, cs:cs + cl], in_=b16[:, cs:cs + cl], func=AF.Exp, scale=-1.0 / 16,
                        bias=sm3[:, BLCp + ch:BLCp + ch + 1]), reads=[F(8), ("blc", p)], writes=[F(11)], tbl="exp")
                    P.op("act", lambda h, ch=ch: h.activation(out=sm3[:, EBLp + ch:EBLp + ch + 1], in_=sm3[:, BLCp + ch:BLCp + ch + 1],
                                                              func=AF.Exp), reads=[("blc", p)], writes=[("ebl", p)], tbl="exp")
                if last:
                    bls = sm3[:, BLCp + 4:BLCp + 20]
                    ebls = sm3[:, EBLp + 4:EBLp + 20]
                    P.op("dve", lambda h: h.tensor_scalar(out=bls, in0=b16[:, 128:256].rearrange("p (s k) -> p s k", k=8)[:, :, 7],
                                                          scalar1=1.0 / 16, scalar2=None, op0=ALU.mult), reads=[F(8)], writes=[("blc", p)])
                    P.op("act", lambda h: h.activation(out=ebls, in_=bls, func=AF.Exp), reads=[("blc", p)], writes=[("ebl", p)], tbl="exp")
                    P.op("dve", lambda h: h.tensor_tensor(out=ek[:, 128:256].rearrange("p (s k) -> p s k", k=8),
                                                          in0=enb[:, 128:256].rearrange("p (s k) -> p s k", k=8),
                                                          in1=ebls.unsqueeze(2).to_broadcast([128, 16, 8]), op=ALU.mult),
                         reads=[F(10), ("ebl", p)], writes=[F(11)])
                psq, psqr = u_mm(8 + p, ub=1)
                P.op("dve", lambda h: h.scalar_tensor_tensor(out=qsA[0:64, :n], in0=psq[0:64, :n], scalar=0.125, in1=eb[0:64, :n],
                                                             op0=ALU.mult, op1=ALU.mult), reads=[psqr, F(9)], writes=[B(1)])
                P.op("dve", lambda h: h.scalar_tensor_tensor(out=qsB[64:128, :n], in0=psq[64:128, :n], scalar=0.125,
                                                             in1=eb[64:128, :n], op0=ALU.mult, op1=ALU.mult),
                     reads=[psqr, F(9)], writes=[B(2)])
                psk, pskr = u_mm(10 + p, ub=1)
                P.op("dve", lambda h: h.tensor_tensor(out=ks[:, :n], in0=psk[:, :n], in1=enb[:, :n], op=ALU.mult),
                     reads=[pskr, F(10)], writes=[B(3)])
                P.op("dve", lambda h: h.tensor_tensor(out=kend[:, :n], in0=psk[:, :n], in1=ek[:, :n], op=ALU.mult),
                     reads=[pskr, F(11)], writes=[B(4)])
                for ch, (cs, cl) in enumerate(vchunks):
                    P.op("pe", lambda h, ch=ch, cs=cs, cl=cl: h.transpose(out=mVb[:cl, ch * 128:(ch + 1) * 128],
                                                                          in_=kend[:, cs:cs + cl], identity=ident),
                         reads=[B(4), "cst"], writes=["mV"], self_sync=False)
                c0_ = 0
                if vchunks[0][1] < 128:
                    cl0 = vchunks[0][1]
                    P.op("act", lambda h: h.activation(out=kendT[p][:cl0, 0, :], in_=mVb[:cl0, 0:128], func=AF.Copy),
                         reads=["mV"], writes=[("kendT", p)])
                    c0_ = 1
                P.op("act", lambda h: h.activation(out=kendT[p][:, c0_:len(vchunks), :],
                                                   in_=mVb[:, c0_ * 128:len(vchunks) * 128].rearrange("p (c v) -> p c v", v=128),
                                                   func=AF.Copy), reads=["mV"], writes=[("kendT", p)])
                for ch, (cs, cl) in enumerate(chunks):
                    for hd in range(2):
                        h4 = 2 * p + hd
                        qs = qsA if hd == 0 else qsB
                        qr = B(1) if hd == 0 else B(2)
                        ab = mcnt["att"] % 2
                        mcnt["att"] += 1
                        gb2 = mcnt["g"] % 2
                        mcnt["g"] += 1
                        attb, attr = ((mG[1], ("mG", 1)), (mV, "mV"))[gb2]
                        P.op("pe", lambda h, cs=cs, cl=cl, qs=qs, attb=attb: h.matmul(
                            attb[:cl, :cl], lhsT=ks[:, cs:cs + cl], rhs=qs[:, cs:cs + cl], start=True, stop=True),
                            reads=[B(3), qr], writes=[attr], self_sync=False)
                        P.op("dve", lambda h, cl=cl, ab=ab, attb=attb: h.tensor_tensor(
                            out=attT[:cl, ab, :cl], in0=attb[:cl, :cl], in1=maskT[:cl, :cl], op=ALU.mult),
                            reads=[attr, "cst"], writes=[("attT", ab)])
                        P.op("pe", lambda h, cs=cs, cl=cl, qs=qs, hd=hd: h.matmul(
                            mO[hd][:, cs:cs + cl], lhsT=Sb[:, p, :], rhs=qs[:, cs:cs + cl], start=True, stop=False),
                            reads=["Sb", qr], writes=[("mO", hd)], self_sync=False)
                        P.op("pe", lambda h, cs=cs, cl=cl, ch=ch, ab=ab, hd=hd, h4=h4: h.matmul(
                            mO[hd][:, cs:cs + cl], lhsT=vtm[:cl, ch, h4 * 128:(h4 + 1) * 128], rhs=attT[:cl, ab, :cl],
                            start=False, stop=True),
                            reads=[("vtm", ch), ("attT", ab)], writes=[("mO", hd)], self_sync=False)
                    P.op("pe", lambda h, cl=cl, ch=ch: h.matmul(
                        psN[:, 0:256], lhsT=kendT[p][:cl, ch, :], rhs=vtm[:cl, ch, p * 256:(p + 1) * 256],
                        start=True, stop=True), reads=[("kendT", p), ("vtm", ch)], writes=["psN"], self_sync=False)
                    for hd in range(2):
                        r0, r1 = hd * 64, hd * 64 + 64
                        P.op("dve", lambda h, ch=ch, hd=hd, r0=r0, r1=r1: h.scalar_tensor_tensor(
                            out=Sf[r0:r1, p, :], in0=Sf[r0:r1, p, :], scalar=sm3[r0:r1, EBLp + ch:EBLp + ch + 1],
                            in1=psN[r0:r1, hd * 128:(hd + 1) * 128], op0=ALU.mult, op1=ALU.add),
                            reads=["Sf", ("ebl", p), "psN"], writes=["Sf"])
                    P.op("act", lambda h: h.activation(out=Sb[:, p, :], in_=Sf[:, p, :], func=AF.Copy), reads=["Sf"], writes=["Sb"])
                if last:
                    P.op("sp", lambda h: h.dma_start(out=So_d[l, p * 128:(p + 1) * 128, :], in_=Sf[:, p, :]),
                         reads=["Sf"], dma_sem=osem("Sf"))
                    cs, cl, ch = 128, 128, 1
                    for hd in range(2):
                        h4 = 2 * p + hd
                        qs = qsA if hd == 0 else qsB
                        qr = B(1) if hd == 0 else B(2)
                        ab = mcnt["att"] % 2
                        mcnt["att"] += 1
                        gb2 = mcnt["g"] % 2
                        mcnt["g"] += 1
                        attb, attr = ((mG[1], ("mG", 1)), (mV, "mV"))[gb2]
                        P.op("pe", lambda h, qs=qs, attb=attb: h.matmul(
                            attb[:, :128], lhsT=ks[:, 128:256], rhs=qs[:, 128:256], start=True, stop=True),
                            reads=[B(3), qr], writes=[attr], self_sync=False)
                        P.op("dve", lambda h, ab=ab, attb=attb: h.tensor_tensor(
                            out=attT[:, ab, :], in0=attb[:, :128], in1=smask, op=ALU.mult),
                            reads=[attr, "cst"], writes=[("attT", ab)])
                        P.op("pe", lambda h, ab=ab, hd=hd, h4=h4: h.matmul(
                            mO[hd][:, 128:256], lhsT=vtm[:, 1, h4 * 128:(h4 + 1) * 128], rhs=attT[:, ab, :],
                            start=True, stop=False, skip_group_check=True),
                            reads=[("vtm", 1), ("attT", ab)], writes=[("mO", hd)], self_sync=False)
                    for hf in range(2):
                        P.op("sp", lambda h, hf=hf: h.dma_start(
                            out=S0f[:].rearrange("p s v -> p (s v)"), in_=S0_d[l, p, hf]),
                            writes=["S0f"], dma_sem=osem("S0ld"))
                        P.op("act", lambda h: h.activation(out=S0b[:], in_=S0f[:], func=AF.Copy), reads=["S0f"], writes=["S0b"])
                        for sq_ in range(8):
                            sg_ = hf * 8 + sq_
                            for hd in range(2):
                                qs = qsA if hd == 0 else qsB
                                qr = B(1) if hd == 0 else B(2)
                                lastmm = (hf == 1 and sq_ == 7)
                                P.op("pe", lambda h, sq_=sq_, sg_=sg_, qs=qs, hd=hd, lastmm=lastmm: h.matmul(
                                    mO[hd][:, 128 + sg_ * 8:128 + sg_ * 8 + 8], lhsT=S0b[:, sq_, :],
                                    rhs=qs[:, 128 + sg_ * 8:128 + sg_ * 8 + 8], start=False, stop=lastmm, skip_group_check=True),
                                    reads=["S0b", qr], writes=[("mO", hd)], self_sync=False)
                    for hf in range(2):
                        P.op("sp", lambda h, hf=hf: h.dma_start(
                            out=S0f[:].rearrange("p s v -> p (s v)"), in_=S0_d[l, p, hf]),
                            writes=["S0f"], dma_sem=osem("S0ld"))
                        P.op("dve", lambda h, hf=hf: h.tensor_tensor(
                            out=kblk[:], in0=kendT[p][:, 1:2, :].to_broadcast([128, 8, 128]),
                            in1=blk16[:, hf * 8:(hf + 1) * 8].unsqueeze(2).to_broadcast([128, 8, 128]), op=ALU.mult),
                            reads=[("kendT", p), "cst"], writes=["S0b"])
                        for sq_ in range(8):
                            sg_ = hf * 8 + sq_
                            ub_, ur_ = ((psN, "psN"), (mV, "mV"))[sq_ % 2]
                            P.op("pe", lambda h, sq_=sq_, ub_=ub_: h.matmul(
                                ub_[:, 0:256], lhsT=kblk[:, sq_, :], rhs=vtm[:, 1, p * 256:(p + 1) * 256],
                                start=True, stop=True), reads=["S0b", ("vtm", 1)], writes=[ur_], self_sync=False)
                            for hd in range(2):
                                r0, r1 = hd * 64, hd * 64 + 64
                                P.op("dve", lambda h, sq_=sq_, sg_=sg_, hd=hd, r0=r0, r1=r1, ub_=ub_: h.scalar_tensor_tensor(
                                    out=S0f[r0:r1, sq_, :], in0=S0f[r0:r1, sq_, :],
                                    scalar=sm3[r0:r1, EBLp + 4 + sg_:EBLp + 5 + sg_],
                                    in1=ub_[r0:r1, hd * 128:(hd + 1) * 128], op0=ALU.mult, op1=ALU.add),
                                    reads=["S0f", ("ebl", p), ur_], writes=["S0f"])
                        P.op("sp", lambda h, hf=hf: h.dma_start(
                            out=Sos_d[l, p, hf], in_=S0f[:].rearrange("p s v -> p (s v)")), reads=["S0f"], dma_sem=osem("S0st"))
                for hd in range(2):
                    h4 = 2 * p + hd
                    osq, rst, sgo, t1 = Bb[5], Fb[9], Fb[10], Fb[7]
                    P.op("act", lambda h, hd=hd: h.activation(out=osq[:, :n], in_=mO[hd][:, :n], func=AF.Square),
                         reads=[("mO", hd)], writes=[B(5)])
                    P.op("pe", lambda h: h.matmul(psN[:, :n], lhsT=ones[:], rhs=osq[:, :n], start=True, stop=True),
                         reads=[B(5), "ones"], writes=["psN"], self_sync=False)
                    P.op("act", lambda h: h.activation(out=rst[:, :n], in_=psN[:, :n], func=AF.Sqrt, scale=1.0 / 128, bias=EPS),
                         reads=["psN"], writes=[F(9)], tbl="sqrt")
                    P.op("dve", lambda h: h.reciprocal(out=rst[:, :n], in_=rst[:, :n]), reads=[F(9)], writes=[F(9)])
                    goidx = [14, 17, 18, 19][h4]
                    psg, psgr = u_mm(goidx, ub=1)
                    P.op("act", lambda h, psg=psg: h.activation(out=sgo[:, :n], in_=psg[:, :n], func=AF.Tanh, scale=0.5),
                         reads=[psgr], writes=[F(10)], tbl="exp")
                    P.op("dve", lambda h, psg=psg: h.scalar_tensor_tensor(out=sgo[:, :n], in0=sgo[:, :n], scalar=1.0, in1=psg[:, :n],
                                                                         op0=ALU.add, op1=ALU.mult),
                         reads=[psgr, F(10)], writes=[F(10)])
                    P.op("dve", lambda h, hd=hd: h.scalar_tensor_tensor(out=t1[:, :n], in0=mO[hd][:, :n], scalar=0.5, in1=rst[:, :n],
                                                                       op0=ALU.mult, op1=ALU.mult),
                         reads=[("mO", hd), F(9)], writes=[F(7)])
                    P.op("dve", lambda h, h4=h4: h.scalar_tensor_tensor(
                        out=mixin[:, 4 + h4, :n], in0=t1[:, :n], scalar=cols[:, cb + 58:cb + 59], in1=sgo[:, :n],
                        op0=ALU.mult, op1=ALU.mult), reads=[F(7), F(10), "cols"], writes=[("mixin", 4 + h4)])

            for c_ in range(4):
                lru_chunk(c_)
            for p_ in range(2):
                gla_pair(p_)

            if last:
                released(mb + 7, extra_reads=["S0f"])
            for mc in range(KC):
                ub = mcnt["u"] % 2
                mcnt["u"] += 1
                for ic in range(KC):
                    sl_, pos = divmod(ic, 3)
                    rsl = (mb + 7 + sl_) % R
                    P.op("pe", lambda h, mc=mc, ic=ic, rsl=rsl, pos=pos, ub=ub: h.matmul(
                        mU[ub][:, :n], lhsT=ring[:, rsl, pos * 1024 + mc * 128:pos * 1024 + (mc + 1) * 128],
                        rhs=mixin[:, ic, :n], start=(ic == 0), stop=(ic == KC - 1)),
                        reads=[("slot", rsl), ("mixin", ic)], writes=[("mU", ub)], self_sync=False)
                P.op("dve", lambda h, mc=mc, ub=ub: h.tensor_tensor(out=x[:, mc, s:s + n], in0=mU[ub][:, :n],
                                                                     in1=x[:, mc, s:s + n], op=ALU.add),
                     reads=[("mU", ub), ("x", mc, ti)], writes=[("x", mc, ti)])


        n_issue = cfg.get("n_slots_used", nslots_total)
        issued = [0]

        def released(n_done, extra_reads=()):
            while issued[0] < min(n_issue, n_done + R):
                issue_slot(issued[0], extra_reads)
                issued[0] += 1

        released(0)

        if do_mixer:
            mixer_setup()
        else:
            P.seal(in_sem)
        obuf = scr[:, 13376:14400].rearrange("p (b n) -> p b n", b=2)

        def final_tile(ti):
            s, n = TILES[ti]
            if final_norm:
                for kc in range(KC):
                    sqa, sqr = ffn_sq[cnt["sq"] % 2]
                    cnt["sq"] += 1
                    P.op("act", lambda h, kc=kc, sqa=sqa: h.activation(out=sqa[:, :n], in_=x[:, kc, s:s + n], func=AF.Square),
                         reads=[("x", kc, ti)], writes=[sqr])
                    P.op("pe", lambda h, kc=kc, sqa=sqa: h.matmul(psN[:, :n], lhsT=ones[:], rhs=sqa[:, :n],
                                                                  start=(kc == 0), stop=(kc == KC - 1)),
                         reads=[sqr, "ones"], writes=["psN"], self_sync=False)
                rsa, rsr = ffn_rs[cnt["rs"] % 2]
                cnt["rs"] += 1
                P.op("act", lambda h: h.activation(out=rsa[:, :n], in_=psN[:, :n], func=AF.Sqrt, scale=1.0 / D, bias=EPS),
                     reads=["psN"], writes=[rsr])
                P.op("dve", lambda h: h.reciprocal(out=rsa[:, :n], in_=rsa[:, :n]), reads=[rsr], writes=[rsr])
                for kc in range(KC):
                    ob = cnt["ob"] % 2
                    cnt["ob"] += 1
                    P.op("dve", lambda h, kc=kc, ob=ob: h.scalar_tensor_tensor(
                        out=obuf[:, ob, :n], in0=x[:, kc, s:s + n], scalar=cols[:, 128 + kc:129 + kc],
                        in1=rsa[:, :n], op0=ALU.mult, op1=ALU.mult),
                        reads=[("x", kc, ti), rsr, "cols"], writes=[("ob", ob)])
                    P.op("sp", lambda h, kc=kc, ob=ob: h.dma_start(out=yT_d[kc * 128:(kc + 1) * 128, s:s + n], in_=obuf[:, ob, :n]),
                         reads=[("ob", ob)], dma_sem=ob_sem[ob])
            else:
                for kc in range(KC):
                    P.op("sp", lambda h, kc=kc: h.dma_start(out=yT_d[kc * 128:(kc + 1) * 128, s:s + n], in_=x[:, kc, s:s + n]),
                         reads=[("x", kc, ti)], dma_sem=ob_sem[kc % 2])

        final_done = False
        for l in range(L):
            base = l * PER_LAYER
            prenormed = (l > 0 and do_ffn2)
            if not do_mixer and not do_ffn2 and l == L - 1:
                ffn(l, 0, base, do_norm=not prenormed, after_tile=final_tile)
                final_done = True
                continue
            ffn(l, 0, base, do_norm=not prenormed)
            full_barrier()
            if do_mixer:
                mixer(l, base + NJ)
                full_barrier()
            if do_ffn2:
                if l + 1 < L:
                    nxt = (lambda ti, l=l: ffn_norm(l + 1, 0, ti))
                else:
                    nxt = final_tile
                    final_done = True
                ffn(l, 1, base + NJ + NMIX, do_norm=True, after_tile=nxt)
        if not final_done:
            full_barrier()
            for ti in range(len(TILES)):
                final_tile(ti)

        P.final_wait("sp", ob_sem + list(osems.values()))
        P.final_wait("pool", slot_sem)
        P.emit({"pe": block.tensor, "act": block.scalar, "dve": block.vector, "pool": block.gpsimd, "sp": block.sync})
    return nc


def _chunk_cols(w, c0, ncols=128):
    return w[:, c0:c0 + ncols].reshape(KC, 128, ncols).transpose(1, 0, 2)


def build_wslots(inp):
    ws = np.zeros((2 * PER_LAYER, 128, SLOT), np.float32)
    for l in range(2):
        base = l * PER_LAYER
        for f, (wgu, wdn) in enumerate([(inp["w_ffn1_gu"][l], inp["w_ffn1_down"][l]),
                                        (inp["w_ffn2_gu"][l], inp["w_ffn2_down"][l])]):
            b0 = base + (0 if f == 0 else NJ + NMIX)
            for j in range(NJ):
                ws[b0 + j, :, 0:1024] = _chunk_cols(wgu, j * 128).reshape(128, 1024)
                ws[b0 + j, :, 1024:2048] = _chunk_cols(wgu, DFF + j * 128).reshape(128, 1024)
                ws[b0 + j, :, 2048:3072] = wdn[j * 128:(j + 1) * 128, :]
        win = inp["w_in"][l]
        order = [0, 1, 2, 3, 4, 5, 6, 7, 8, 9, 10, 11, 12, 13, 16, 14, 15, 17, 18, 19]
        mb = base + NJ
        for idx, c in enumerate(order):
            sl, pos = divmod(idx, 3)
            ws[mb + sl, :, pos * 1024:(pos + 1) * 1024] = _chunk_cols(win, c * 128).reshape(128, 1024)
        ws[mb + 6, :, 2048:2048 + 128] = _chunk_cols(win, 2560, 16).reshape(128, 128)
        wo = inp["w_out"][l]
        for ic in range(8):
            sl, pos = divmod(ic, 3)
            ws[mb + 7 + sl, :, pos * 1024:(pos + 1) * 1024] = wo[ic * 128:(ic + 1) * 128, :]
    return ws


def build_cols(inp):
    c = np.zeros((128, NCOLS), np.float32)

    def put(col, vec, nch):
        c[:, col:col + nch] = vec.reshape(nch, 128).T

    for l in range(2):
        b = l * 64
        put(b + 0, inp["norm_ffn1"][l], 8)
        put(b + 8, inp["norm_mix"][l], 8)
        put(b + 16, inp["norm_ffn2"][l], 8)
        for k in range(4):
            put(b + 24 + k * 4, inp["lru_conv_w"][l][k], 4)
        put(b + 40, inp["lru_conv_b"][l], 4)
        put(b + 44, inp["lru_ba"][l], 4)
        put(b + 48, inp["lru_bx"][l], 4)
        put(b + 52, inp["lru_lambda"][l], 4)
        put(b + 56, inp["gla_b_gate"][l], 2)
        put(b + 58, inp["gla_norm"][l], 1)
    put(128, inp["norm_final"], 8)
    return c


def build_consts():
    c = np.zeros((128, NCST), np.float32)
    j = np.arange(128)
    c[:, 0:128] = np.eye(128, dtype=np.float32)
    c[:, 128:256] = (j[:, None] <= j[None, :]).astype(np.float32)
    c[:, 256:384] = ((j[:, None] <= j[None, :]) & (j[:, None] // 8 == j[None, :] // 8)).astype(np.float32)
    c[:, 384:400] = (j[:, None] // 8 == np.arange(16)[None, :]).astype(np.float32)
    c[:, 400:1040] = (np.arange(640) % 128 != 0).astype(np.float32)[None, :]
    c[:, 1040:1168] = (np.arange(128) % 8 != 0).astype(np.float32)[None, :]
    return c


_CFG = {}


def make_in_maps(inp):
    ws = build_wslots(inp)
    colsv = build_cols(inp)
    consts = build_consts()
    wab = np.ascontiguousarray(np.stack([inp["lru_wa"], inp["lru_wx"]], axis=1))
    metaT = inp["meta"].T
    in_maps = []
    for c in range(NCORES):
        xs = inp["x_sample"][16 * c:16 * c + 16].reshape(128, D)
        xT = np.ascontiguousarray(np.concatenate([metaT, inp["x_prompt"][c].T, xs.T], axis=1))
        sl = slice(16 * c, 16 * c + 16)
        in_maps.append({
            "xT": xT, "wslots": ws, "cols": colsv, "wab": wab, "consts": consts,
            "wg2": np.ascontiguousarray(inp["gla_w_gate2"]),
            "h0T": np.ascontiguousarray(inp["state_lru_h"][:, sl].transpose(0, 2, 1)),
            "c0T": np.ascontiguousarray(inp["state_lru_conv"][:, sl].transpose(0, 3, 1, 2)),
            "S0": np.ascontiguousarray(inp["state_gla_S"][:, sl].reshape(2, 2, 8, 2, 128, 128)
                                       .transpose(0, 3, 1, 4, 2, 5)).reshape(2, 2, 2, 128, 1024),
        })
    return in_maps


def kernel(**inputs):
    inp = {k: np.asarray(v) for k, v in inputs.items()}
    nc = build_program(dict(_CFG))
    in_maps = make_in_maps(inp)
    res = run_bass_kernel_spmd(nc, in_maps, core_ids=list(range(NCORES)))
    outs = res.results
    y_prompt = np.stack([outs[c]["yT"][:, 16:NPROMPT].T for c in range(NCORES)])
    y_sample = np.concatenate([outs[c]["yT"][:, NPROMPT:].T.reshape(16, 8, D) for c in range(NCORES)])
    hp = np.stack([outs[c]["hTo"][:, :, 0] for c in range(NCORES)], axis=1)
    hs = np.concatenate([outs[c]["hTo"][:, :, 1:].transpose(0, 2, 1) for c in range(NCORES)], axis=1)
    cp = np.stack([outs[c]["cTo"][:, :, 0, :].transpose(0, 2, 1) for c in range(NCORES)], axis=1)
    cs = np.concatenate([outs[c]["cTo"][:, :, 1:, :].transpose(0, 2, 3, 1) for c in range(NCORES)], axis=1)
    Sp = np.stack([outs[c]["So"].reshape(2, 4, 64, 128) for c in range(NCORES)], axis=1)
    Ss = np.concatenate([outs[c]["Sos"].reshape(2, 2, 2, 128, 8, 128).transpose(0, 2, 4, 1, 3, 5).reshape(2, 16, 4, 64, 128)
                         for c in range(NCORES)], axis=1)
    f = lambda a: np.ascontiguousarray(a, dtype=np.float32)
    return (f(y_prompt), f(y_sample), f(hp), f(cp), f(Sp), f(hs), f(cs), f(Ss))
```
